# Optimizing a Trainium2 kernel written in Bass

```python
import math
import jax
import jax.numpy as jnp
from jax import lax
import numpy as np

D_MODEL = 1024
BATCH = 4
SEQ = 4096
DEPTH = 2
DEC_BATCH = 2
DEC_SEQ = 16384
PAST_LEN = 128

HEAD_DIM = 64
N_DIFF_HEADS = 4
DIFF_WIDTH = N_DIFF_HEADS * 2 * HEAD_DIM
N_DIL_HEADS = 8
DIL_WIDTH = N_DIL_HEADS * HEAD_DIM
MIX_WIDTH = DIFF_WIDTH + DIL_WIDTH
IN_WIDTH = 3 * DIFF_WIDTH + 3 * DIL_WIDTH
DIL_PATTERNS = ((128, 1), (512, 4), (2048, 16))
ROPE_THETA = 10000.0
N_MEM = 256
N_CROSS_HEADS = 4
CROSS_HEAD_DIM = D_MODEL // N_CROSS_HEADS
D_FF = 2816
CONV_WIDTH = 3
Q_BLOCK = 128
DIL_Q_BLOCK = 64
RMS_EPS = 1e-6

kernel_name = "hybrid_diff_dilated_encoder"


def rmsnorm(x, g):
    xf = x.astype(jnp.float32)
    y = xf * lax.rsqrt(jnp.mean(xf * xf, axis=-1, keepdims=True) + RMS_EPS)
    return (y * g.astype(jnp.float32)).astype(x.dtype)


def rope_tables(seq, dim):
    inv = 1.0 / (ROPE_THETA ** (jnp.arange(0, dim, 2, dtype=jnp.float32) / dim))
    ang = jnp.arange(seq, dtype=jnp.float32)[:, None] * inv[None, :]
    return jnp.cos(ang), jnp.sin(ang)


def apply_rope(x, cos, sin):
    x1, x2 = jnp.split(x.astype(jnp.float32), 2, axis=-1)
    c = cos[:, None, :]
    s = sin[:, None, :]
    return jnp.concatenate([x1 * c - x2 * s, x2 * c + x1 * s], axis=-1).astype(x.dtype)


def diff_attention(q, k, v, lam, sub_g, lambda_init):
    B, S, H, _, d = q.shape
    nb = S // Q_BLOCK
    scale = d ** -0.5
    qb = q.reshape(B, nb, Q_BLOCK, H, 2, d).transpose(1, 0, 2, 3, 4, 5)

    def block(qblk):
        s = jnp.einsum('bqhcd,bkhcd->bhcqk', qblk, k).astype(jnp.float32) * scale
        p = jax.nn.softmax(s, axis=-1)
        w = p[:, :, 0] - lam * p[:, :, 1]
        return jnp.einsum('bhqk,bkhe->bqhe', w.astype(v.dtype), v)

    o = lax.map(block, qb)
    o = o.transpose(1, 0, 2, 3, 4).reshape(B, S, H, 2 * d)
    o = rmsnorm(o, sub_g) * (1.0 - lambda_init)
    return o.reshape(B, S, H * 2 * d)


def dilated_offsets():
    sizes = [2 * (w // (2 * r)) + 1 for (w, r) in DIL_PATTERNS]
    J = max(sizes)
    offs = np.zeros((len(DIL_PATTERNS), J), np.int32)
    valid = np.zeros((len(DIL_PATTERNS), J), bool)
    for g, (w, r) in enumerate(DIL_PATTERNS):
        n = w // (2 * r)
        o = r * np.arange(-n, n + 1)
        offs[g, :o.shape[0]] = o
        valid[g, :o.shape[0]] = True
    return offs, valid


def dilated_attention(q, k, v):
    B, S, H, d = q.shape
    offs_np, valid_np = dilated_offsets()
    reach = int(np.abs(offs_np).max())
    offs = jnp.asarray(offs_np)
    valid = jnp.asarray(valid_np)
    kp = jnp.pad(k, ((0, 0), (reach, reach), (0, 0), (0, 0)))
    vp = jnp.pad(v, ((0, 0), (reach, reach), (0, 0), (0, 0)))
    nb = S // DIL_Q_BLOCK
    scale = d ** -0.5
    qb = q.reshape(B, nb, DIL_Q_BLOCK, H, d).transpose(1, 0, 2, 3, 4)
    starts = jnp.arange(nb, dtype=jnp.int32) * DIL_Q_BLOCK

    def block(args):
        qblk, start = args
        pos = start + jnp.arange(DIL_Q_BLOCK, dtype=jnp.int32)
        kpos = pos[:, None, None] + offs[None]
        ok = valid[None] & (kpos >= 0) & (kpos < S)
        idx = kpos + reach
        kg = jnp.take(kp, idx, axis=1)
        vg = jnp.take(vp, idx, axis=1)
        s = jnp.einsum('bqhd,bqgjhd->bqghj', qblk, kg).astype(jnp.float32) * scale
        s = jnp.where(ok[None, :, :, None, :], s, -jnp.inf)
        lse = jax.nn.logsumexp(s, axis=-1)
        p = jnp.exp(s - lse[..., None])
        o = jnp.einsum('bqghj,bqgjhd->bqghd', p.astype(v.dtype), vg)
        alpha = jax.nn.softmax(lse, axis=2)
        return jnp.einsum('bqgh,bqghd->bqhd', alpha.astype(o.dtype), o)

    o = lax.map(block, (qb, starts))
    return o.transpose(1, 0, 2, 3, 4).reshape(B, S, H * d)


def memory_cross_attention(h, mem_n, w_q, w_kv, w_o):
    B, S, _ = h.shape
    M = mem_n.shape[1]
    q = (h @ w_q).reshape(B, S, N_CROSS_HEADS, CROSS_HEAD_DIM)
    kv = (mem_n @ w_kv).reshape(B, M, 2, N_CROSS_HEADS, CROSS_HEAD_DIM)
    k = kv[:, :, 0]
    v = kv[:, :, 1]
    s = jnp.einsum('bqhd,bmhd->bhqm', q, k).astype(jnp.float32) * (CROSS_HEAD_DIM ** -0.5)
    p = jax.nn.softmax(s, axis=-1)
    o = jnp.einsum('bhqm,bmhd->bqhd', p.astype(v.dtype), v).reshape(B, S, D_MODEL)
    return o @ w_o


def conv_gated_mlp(h, w_up, conv_w, conv_b, w_down):
    S = h.shape[1]
    u = h @ w_up
    half = CONV_WIDTH // 2
    up = jnp.pad(u, ((0, 0), (half, half), (0, 0)))
    c = conv_b
    for t in range(CONV_WIDTH):
        c = c + up[:, t:t + S] * conv_w[t]
    gate, val = jnp.split(c, 2, axis=-1)
    return (jax.nn.gelu(gate, approximate=True) * val) @ w_down


def encoder_trunk(x, mem, p):
    B, S, _ = x.shape
    cos, sin = rope_tables(S, HEAD_DIM)
    splits = np.cumsum([DIFF_WIDTH, DIFF_WIDTH, DIFF_WIDTH, DIL_WIDTH, DIL_WIDTH]).tolist()
    for l in range(DEPTH):
        lambda_init = 0.8 - 0.6 * math.exp(-0.3 * l)
        h = rmsnorm(x, p['mix_pre_g'][l])
        qa, ka, va, qd, kd, vd = jnp.split(h @ p['w_in'][l], splits, axis=-1)
        qa = apply_rope(qa.reshape(B, S, 2 * N_DIFF_HEADS, HEAD_DIM), cos, sin).reshape(B, S, N_DIFF_HEADS, 2, HEAD_DIM)
        ka = apply_rope(ka.reshape(B, S, 2 * N_DIFF_HEADS, HEAD_DIM), cos, sin).reshape(B, S, N_DIFF_HEADS, 2, HEAD_DIM)
        va = va.reshape(B, S, N_DIFF_HEADS, 2 * HEAD_DIM)
        lam = (jnp.exp(jnp.sum(p['lambda_q1'][l].astype(jnp.float32) * p['lambda_k1'][l].astype(jnp.float32)))
               - jnp.exp(jnp.sum(p['lambda_q2'][l].astype(jnp.float32) * p['lambda_k2'][l].astype(jnp.float32)))
               + lambda_init)
        oa = diff_attention(qa, ka, va, lam, p['diff_subln_g'][l], lambda_init)
        qd = apply_rope(qd.reshape(B, S, N_DIL_HEADS, HEAD_DIM), cos, sin)
        kd = apply_rope(kd.reshape(B, S, N_DIL_HEADS, HEAD_DIM), cos, sin)
        vd = vd.reshape(B, S, N_DIL_HEADS, HEAD_DIM)
        od = dilated_attention(qd, kd, vd)
        mix = jnp.concatenate([oa, od], axis=-1) @ p['w_out'][l]
        x = x + rmsnorm(mix, p['mix_post_g'][l])
        h = rmsnorm(x, p['xattn_pre_g'][l])
        mem_n = rmsnorm(mem, p['mem_norm_g'][l])
        c = memory_cross_attention(h, mem_n, p['w_xq'][l], p['w_xkv'][l], p['w_xo'][l])
        x = x + rmsnorm(c, p['xattn_post_g'][l])
        h = rmsnorm(x, p['ffn_pre_g'][l])
        f = conv_gated_mlp(h, p['w_up'][l], p['conv_w'][l], p['conv_b'][l], p['w_down'][l])
        x = x + rmsnorm(f, p['ffn_post_g'][l])
    return x


def setup_inputs(seed: int = 0) -> dict:
    key = jax.random.key(seed)
    ks = jax.random.split(key, 32)
    f32 = jnp.float32

    def nrm(k, shape, scale):
        return jax.random.normal(k, shape, f32) * scale

    def gain(k):
        return 1.0 + 0.1 * jax.random.normal(k, (DEPTH, D_MODEL), f32)

    return {
        'x_prompt': nrm(ks[0], (BATCH, SEQ, D_MODEL), 1.0),
        'x_sample': nrm(ks[1], (DEC_BATCH, DEC_SEQ, D_MODEL), 1.0),
        'mem_prompt': nrm(ks[2], (BATCH, N_MEM, D_MODEL), 1.0),
        'mem_sample': nrm(ks[3], (DEC_BATCH, N_MEM, D_MODEL), 1.0),
        'w_in': nrm(ks[4], (DEPTH, D_MODEL, IN_WIDTH), D_MODEL ** -0.5),
        'w_out': nrm(ks[5], (DEPTH, MIX_WIDTH, D_MODEL), MIX_WIDTH ** -0.5),
        'lambda_q1': nrm(ks[6], (DEPTH, HEAD_DIM), 0.1),
        'lambda_k1': nrm(ks[7], (DEPTH, HEAD_DIM), 0.1),
        'lambda_q2': nrm(ks[8], (DEPTH, HEAD_DIM), 0.1),
        'lambda_k2': nrm(ks[9], (DEPTH, HEAD_DIM), 0.1),
        'diff_subln_g': 1.0 + 0.1 * jax.random.normal(ks[10], (DEPTH, 2 * HEAD_DIM), f32),
        'w_xq': nrm(ks[11], (DEPTH, D_MODEL, D_MODEL), D_MODEL ** -0.5),
        'w_xkv': nrm(ks[12], (DEPTH, D_MODEL, 2 * D_MODEL), D_MODEL ** -0.5),
        'w_xo': nrm(ks[13], (DEPTH, D_MODEL, D_MODEL), D_MODEL ** -0.5),
        'w_up': nrm(ks[14], (DEPTH, D_MODEL, 2 * D_FF), D_MODEL ** -0.5),
        'conv_w': nrm(ks[15], (DEPTH, CONV_WIDTH, 2 * D_FF), CONV_WIDTH ** -0.5),
        'conv_b': nrm(ks[16], (DEPTH, 2 * D_FF), 0.01),
        'w_down': nrm(ks[17], (DEPTH, D_FF, D_MODEL), D_FF ** -0.5),
        'mix_pre_g': gain(ks[18]),
        'mix_post_g': gain(ks[19]),
        'mem_norm_g': gain(ks[20]),
        'xattn_pre_g': gain(ks[21]),
        'xattn_post_g': gain(ks[22]),
        'ffn_pre_g': gain(ks[23]),
        'ffn_post_g': gain(ks[24]),
    }


def reference(x_prompt, x_sample, mem_prompt, mem_sample, w_in, w_out, lambda_q1, lambda_k1,
              lambda_q2, lambda_k2, diff_subln_g, w_xq, w_xkv, w_xo, w_up, conv_w, conv_b, w_down,
              mix_pre_g, mix_post_g, mem_norm_g, xattn_pre_g, xattn_post_g, ffn_pre_g, ffn_post_g):
    params = {
        'w_in': w_in, 'w_out': w_out,
        'lambda_q1': lambda_q1, 'lambda_k1': lambda_k1,
        'lambda_q2': lambda_q2, 'lambda_k2': lambda_k2,
        'diff_subln_g': diff_subln_g,
        'w_xq': w_xq, 'w_xkv': w_xkv, 'w_xo': w_xo,
        'w_up': w_up, 'conv_w': conv_w, 'conv_b': conv_b, 'w_down': w_down,
        'mix_pre_g': mix_pre_g, 'mix_post_g': mix_post_g, 'mem_norm_g': mem_norm_g,
        'xattn_pre_g': xattn_pre_g, 'xattn_post_g': xattn_post_g,
        'ffn_pre_g': ffn_pre_g, 'ffn_post_g': ffn_post_g,
    }
    y_prompt = encoder_trunk(x_prompt, mem_prompt, params)
    y_sample = encoder_trunk(x_sample, mem_sample, params)
    return (y_prompt, y_sample)
```

```python
import numpy as np
import concourse.bass as bass
import concourse.mybir as mybir
from concourse.bass_utils import run_bass_kernel_spmd

F32 = mybir.dt.float32
BF16 = mybir.dt.bfloat16
AF = mybir.ActivationFunctionType
ALU = mybir.AluOpType
AX = mybir.AxisListType

D = 1024
KC = 8
L = 2
NQK = 2048
NV = 1024
DFF = 2816
NUP = 5632
NPAIR = 22
M = 256
EPS = 1e-6
NEG = -30000.0
FW = 510
STOP = []


class StopBuild(Exception):
    pass


DEAD = [False]


def chk(tag):
    if STOP and STOP[0] == tag:
        DEAD[0] = True


class Agent:
    def __init__(self, sem, step):
        self.sem = sem
        self.step = step
        self.count = 0


class Buf:
    __slots__ = ("w", "r", "x")

    def __init__(self, x=False):
        self.w = {}
        self.r = {}
        self.x = x


def PB():
    return Buf(True)


class Sched:
    def __init__(self, nc, sems, dma_sems):
        self.nc = nc
        self.E = {"pe": nc.tensor, "act": nc.scalar, "dve": nc.vector, "pool": nc.gpsimd, "sp": nc.sync}
        self.ag = {k: Agent(sems[k], 1) for k in self.E}
        self.seen = {k: {} for k in self.E}
        self.pool = [Agent(x, 16) for x in dma_sems]
        self.next = 0

    def dma_agent(self):
        a = self.pool[self.next]
        self.next += 1
        return a

    def op(self, eng, fn, reads=(), writes=(), agent=None):
        if DEAD[0]:
            return None
        ag = agent if agent is not None else self.ag[eng]
        deps = {}

        def add(a, c):
            if deps.get(a, 0) < c:
                deps[a] = c

        for b in reads:
            for a, c in b.w.items():
                add(a, c)
            if b.x:
                for a, c in b.r.items():
                    if a is not ag:
                        add(a, c)
        for b in writes:
            for a, c in b.w.items():
                add(a, c)
            for a, c in b.r.items():
                add(a, c)
        if agent is not None and ag.count > 0:
            add(ag, ag.count)
        seen = self.seen[eng]
        e = self.E[eng]
        for a, c in deps.items():
            if agent is None and a is ag and eng == "pe":
                continue
            if seen.get(a, 0) >= c:
                continue
            e.wait_ge(a.sem, c)
            seen[a] = c
        ins = fn()
        ag.count += ag.step
        ins.then_inc(ag.sem, ag.step)
        for b in reads:
            b.r[ag] = ag.count
        for b in writes:
            b.w[ag] = ag.count
            b.r = {}
        return ins

    def full_barrier(self):
        if DEAD[0]:
            return
        agents = list(self.ag.values()) + [a for a in self.pool if a.count > 0]
        for eng, e in self.E.items():
            for a in agents:
                if a is self.ag[eng] or a.count == 0:
                    continue
                if self.seen[eng].get(a, 0) < a.count:
                    e.wait_ge(a.sem, a.count)
                    self.seen[eng][a] = a.count

    def barrier(self, eng, bufs):
        if DEAD[0]:
            return
        e = self.E[eng]
        for b in bufs:
            for a, c in b.w.items():
                if self.seen[eng].get(a, 0) < c:
                    e.wait_ge(a.sem, c)
                    self.seen[eng][a] = c


def build(T):
    NB = T // 512
    NKC = T // 128
    NFB = (T + FW - 1) // FW
    PAD = 64
    HW = ((PAD - 1 + FW * (NFB - 1) + 512 + 63) // 64) * 64
    SEG = T // 4
    assert SEG % 512 == 0
    nc = bass.Bass("TRN2", target_bir_lowering=False)

    def din(name, shape, dt=F32):
        return nc.dram_tensor(name, list(shape), dt, kind="ExternalInput").ap()

    def dscr(name, shape, dt):
        return nc.dram_tensor(name, list(shape), dt, kind="Internal").ap()

    xT_in = din("xT", [D, T])
    memT_in = din("memT", [4, D, M])
    rope_c = din("rope_c", [128, T])
    rope_s = din("rope_s", [128, T])
    masks_in = din("masks", [128, 20 * 512])
    dbias_in = din("dbias", [128, NB * 20])
    dbias1_in = din("dbias1", [128, NB * NKC])
    segflag_in = din("segflag", [128, 1])
    gains_in = din("gains", [128, L * 7 * 8])
    convw_in = din("convw", [128, L * 3 * 44])
    convb_in = din("convb", [128, L * 44])
    subg_in = din("subg", [128, L])
    lam_in = din("lamv", [128, L * 4 * 64])
    pm_in = din("pmat", [128, 128])
    id_in = din("ident", [128, 128])
    w_qk = din("w_qk", [L, D, NQK])
    w_v = din("w_v", [L, D, NV])
    w_out = din("w_out", [L, D, D])
    w_xq = din("w_xq", [L, D, D])
    w_xkv = din("w_xkv", [L, D, 2 * D])
    w_xo = din("w_xo", [L, D, D])
    w_up = din("w_up", [L, D, NUP])
    w_down = din("w_down", [L, DFF, D])
    yT = nc.dram_tensor("yT", [D, T], F32, kind="ExternalOutput").ap()

    qkT = dscr("qkT", [NQK, T], BF16)
    vtok = dscr("vtok", [T, NV], BF16)
    oT = dscr("oT", [D, T], BF16)
    xmid = dscr("xmid", [D, T], F32)
    x1 = dscr("x1", [D, T], F32)
    h3T = dscr("h3T", [D, HW], BF16)
    wupb = dscr("wupb", [D, NUP], BF16)

    import contextlib

    with contextlib.ExitStack() as top:
        sems = {k: top.enter_context(nc.semaphore("s_" + k)) for k in ["pe", "act", "dve", "pool", "sp"]}
        dsems = [top.enter_context(nc.semaphore("d%d" % i)) for i in range(20)]
        S = Sched(nc, sems, dsems)
        pe, act, dve, pool, sp = nc.tensor, nc.scalar, nc.vector, nc.gpsimd, nc.sync

        uid = [0]

        def sb(es, name, shape, dt):
            uid[0] += 1
            return es.enter_context(nc.sbuf_tensor("s%d_%s" % (uid[0], name), list(shape), dt))

        def ps(es, name, shape, dt=F32):
            uid[0] += 1
            return es.enter_context(nc.psum_tensor("p%d_%s" % (uid[0], name), list(shape), dt))

        def load(dst_ap, src_ap, dstbuf, agent, srcbufs=()):
            S.op("sp", lambda: sp.dma_start(out=dst_ap, in_=src_ap), reads=list(srcbufs), writes=[dstbuf], agent=agent)

        def store(dst_ap, src_ap, srcbuf, agent, dstbufs=()):
            S.op("sp", lambda: sp.dma_start(out=dst_ap, in_=src_ap), reads=[srcbuf], writes=list(dstbufs), agent=agent)

        ones = sb(top, "ones", [128, 128], BF16)
        pmat = sb(top, "pmat", [128, 128], BF16)
        pmf = sb(top, "pmf", [128, 128], F32)
        identb = sb(top, "identb", [128, 128], BF16)
        idf = sb(top, "idf", [128, 128], F32)
        epst = sb(top, "epst", [128, 1], F32)
        gains = sb(top, "gains", [128, L * 7 * 8], F32)
        convw = sb(top, "convw", [128, L * 3 * 44], F32)
        convb = sb(top, "convb", [128, L * 44], F32)
        subg = sb(top, "subg", [128, L], F32)
        lamv = sb(top, "lamv", [128, L * 4 * 64], F32)
        neglam = sb(top, "neglam", [128, L], F32)
        lamtmp = sb(top, "lamtmp", [128, 64], F32)
        lame = sb(top, "lame", [128, 4], F32)
        segflag = sb(top, "segflag", [128, 1], F32)
        negflag = sb(top, "negflag", [128, 1], F32)
        dbias = sb(top, "dbias", [128, NB * 20], F32)
        B_const = Buf()
        a_const = S.dma_agent()
        for t_, src in [(gains, gains_in), (convw, convw_in), (convb, convb_in), (subg, subg_in), (lamv, lam_in),
                        (segflag, segflag_in), (dbias, dbias_in), (pmf, pm_in), (idf, id_in)]:
            load(t_[:], src, B_const, a_const)
        B_c2 = Buf()
        S.op("pool", lambda: pool.memset(ones[:], 1.0), writes=[B_c2])
        S.op("pool", lambda: pool.memset(epst[:], EPS), writes=[B_c2])
        S.op("pool", lambda: pool.tensor_copy(out=pmat[:], in_=pmf[:]), reads=[B_const], writes=[B_c2])
        S.op("pool", lambda: pool.tensor_copy(out=identb[:], in_=idf[:]), reads=[B_const], writes=[B_c2])
        B_lam = Buf()
        for l in range(L):
            lam_init = 0.8 - 0.6 * float(np.exp(-0.3 * l))
            for j in range(2):
                a_ = lamv[:, (l * 4 + 2 * j) * 64:(l * 4 + 2 * j + 1) * 64]
                b_ = lamv[:, (l * 4 + 2 * j + 1) * 64:(l * 4 + 2 * j + 2) * 64]
                S.op("dve", lambda: dve.tensor_tensor(out=lamtmp[:], in0=a_, in1=b_, op=ALU.mult), reads=[B_const, B_lam], writes=[B_lam])
                S.op("dve", lambda: dve.tensor_reduce(out=lame[:, 2 * l + j:2 * l + j + 1], in_=lamtmp[:], axis=AX.X, op=ALU.add), reads=[B_lam], writes=[B_lam])
            S.op("act", lambda: act.activation(out=lame[:, 2 * l:2 * l + 2], in_=lame[:, 2 * l:2 * l + 2], func=AF.Exp), reads=[B_lam], writes=[B_lam])
            S.op("dve", lambda: dve.scalar_tensor_tensor(out=neglam[:, l:l + 1], in0=lame[:, 2 * l + 1:2 * l + 2], scalar=-lam_init,
                                                          in1=lame[:, 2 * l:2 * l + 1], op0=ALU.add, op1=ALU.subtract), reads=[B_lam], writes=[B_lam])
            S.op("dve", lambda: dve.tensor_scalar(out=subg[:, l:l + 1], in0=subg[:, l:l + 1], scalar1=1.0 - lam_init, scalar2=None, op0=ALU.mult),
                 reads=[B_const, B_lam], writes=[B_lam])
        S.op("dve", lambda: dve.tensor_scalar(out=negflag[:], in0=segflag[:], scalar1=-1.0, scalar2=None, op0=ALU.mult), reads=[B_const, B_lam], writes=[B_lam])
        CONST = [B_const, B_c2, B_lam]

        def gcol(l, which, kc):
            i = (l * 7 + which) * 8 + kc
            return gains[:, i:i + 1]

        h3v = h3T.rearrange("(kc p) t -> p kc t", p=128)
        B_h3pad = Buf()
        with contextlib.ExitStack() as es:
            z = sb(es, "zt", [128, 8, 512], BF16)
            Bz = Buf()
            S.op("pool", lambda: pool.memset(z[:], 0.0), writes=[Bz])
            az = S.dma_agent()
            store(h3v[:, :, 0:PAD], z[:, :, 0:PAD], Bz, az, [B_h3pad])
            c0 = PAD + T
            while c0 < HW:
                cw = min(512, HW - c0)
                store(h3v[:, :, c0:c0 + cw], z[:, :, 0:cw], Bz, az, [B_h3pad])
                c0 += cw
            S.barrier("sp", [B_h3pad])

        cast_rr = [0]

        def load_weight(dst, src, rows, cols, Bdst, stg, stgB, stgA, SW):
            nr = rows // 128
            srcv = src.rearrange("(kc p) n -> p kc n", p=128)
            for kc in range(nr):
                for c0 in range(0, cols, SW):
                    cw = min(SW, cols - c0)
                    i = cast_rr[0] % 2
                    cast_rr[0] += 1
                    load(stg[i][:, 0:cw], srcv[:, kc, c0:c0 + cw], stgB[i], stgA[i])
                    eng = ["pool", "dve"][i]
                    E = S.E[eng]
                    S.op(eng, lambda: E.tensor_copy(out=dst[:, kc, c0:c0 + cw], in_=stg[i][:, 0:cw]), reads=[stgB[i]], writes=[Bdst[i]])

        def rmsnorm_T(src, w, gl, gwhich, hT, sq, ssp, rstd, Bsrc, Bh, Bsq, Bss, Brs, nfree):
            S.op("act", lambda: act.activation(out=sq[:, :, 0:w], in_=src[:, :, 0:w], func=AF.Square), reads=Bsrc, writes=[Bsq])

            def mm():
                for kc in range(KC):
                    ins = pe.matmul(ssp[:, 0:w], lhsT=ones[:], rhs=sq[:, kc, 0:w], start=(kc == 0), stop=(kc == KC - 1))
                return ins
            S.op("pe", mm, reads=[Bsq] + CONST, writes=[Bss])
            S.op("act", lambda: act.activation(out=rstd[:, 0:w], in_=ssp[:, 0:w], func=AF.Ln, bias=epst[:], scale=1.0 / nfree), reads=[Bss] + CONST, writes=[Brs])
            S.op("act", lambda: act.activation(out=rstd[:, 0:w], in_=rstd[:, 0:w], func=AF.Exp, scale=-0.5), reads=[Brs], writes=[Brs])
            if hT is not None:
                for kc in range(KC):
                    k2 = 0
                    eng = "dve"
                    E = S.E[eng]
                    S.op(eng, lambda: E.scalar_tensor_tensor(out=hT[:, kc, 0:w], in0=src[:, kc, 0:w], scalar=gcol(gl, gwhich, kc), in1=rstd[:, 0:w],
                                                             op0=ALU.mult, op1=ALU.mult), reads=Bsrc + [Brs] + CONST, writes=[Bh[k2]])

        B_xdst_prev = []
        try:
            for l in range(L):
                chk('start')
                x_src = xT_in if l == 0 else x1
                x_dst = x1 if l == 0 else yT
                xsv = x_src.rearrange("(kc p) t -> p kc t", p=128)
                xdv = x_dst.rearrange("(kc p) t -> p kc t", p=128)
                xmv = xmid.rearrange("(kc p) t -> p kc t", p=128)
                qkv_ = qkT.rearrange("(fc p) t -> p fc t", p=128)
                oTv = oT.rearrange("(kc p) t -> p kc t", p=128)
                vtv = vtok.rearrange("(s p) n -> p s n", p=128)
                B_qk = [Buf() for _ in range(NB)]
                B_v = [Buf() for _ in range(NB)]
                B_oT = [Buf() for _ in range(NB)]
                B_xmid = [Buf() for _ in range(NB)]
                B_h3 = [Buf() for _ in range(NB)]
                B_xdst = Buf()
                B_wup = Buf()

                S.full_barrier()
                S.next = 2
                with contextlib.ExitStack() as es:
                    Wqk = sb(es, "Wqk", [128, KC, NQK], BF16)
                    Wv = sb(es, "Wv", [128, KC, NV], BF16)
                    stg = [sb(es, "stg%d" % i, [128, 2048], F32) for i in range(2)]
                    stgB = [Buf(), Buf()]
                    stgA = [S.dma_agent(), S.dma_agent()]
                    BW = [Buf(), Buf()]
                    load_weight(Wqk, w_qk[l], D, NQK, BW, stg, stgB, stgA, 2048)
                    load_weight(Wv, w_v[l], D, NV, BW, stg, stgB, stgA, 2048)
                    chk('A0')
                    xb = [sb(es, "xb%d" % i, [128, KC, 512], F32) for i in range(2)]
                    cb = [sb(es, "cb%d" % i, [128, 512], F32) for i in range(2)]
                    sbt = [sb(es, "sbt%d" % i, [128, 512], F32) for i in range(2)]
                    Bxb = [Buf(), Buf()]
                    Bcb = [Buf(), Buf()]
                    Axb = [S.dma_agent(), S.dma_agent()]
                    Acb = [S.dma_agent(), S.dma_agent()]
                    sq = sb(es, "sq", [128, KC, 512], BF16)
                    rstd = sb(es, "rstd", [128, 512], F32)
                    hT = sb(es, "hT", [128, KC, 512], BF16)
                    t1 = [sb(es, "t1_%d" % i, [128, 512], F32) for i in range(2)]
                    t2 = [sb(es, "t2_%d" % i, [128, 512], F32) for i in range(2)]
                    qbf = [sb(es, "qbf%d" % i, [128, 512], BF16) for i in range(2)]
                    qko = sb(es, "qko", [128, 16, 512], BF16)
                    vo = sb(es, "vo", [128, 4, NV], BF16)
                    Bsq, Brs = Buf(), Buf()
                    Bh = [Buf(), Buf()]
                    Bt1 = [Buf(), Buf()]
                    Bt2 = [Buf(), Buf()]
                    Bqbf = [Buf(), Buf()]
                    Bqko, Bvo = Buf(), Buf()
                    Aqko, Avo = S.dma_agent(), S.dma_agent()
                    ssp = ps(es, "ssp", [128, 512])
                    pa = [ps(es, "pa%d" % i, [128, 512]) for i in range(2)]
                    pb = [ps(es, "pb%d" % i, [128, 512]) for i in range(2)]
                    pv = [ps(es, "pv%d" % i, [128, 512]) for i in range(2)]
                    Bss = PB()
                    Bpa = [PB(), PB()]
                    Bpb = [PB(), PB()]
                    Bpv = [PB(), PB()]

                    def issue_loads(b):
                        i = b % 2
                        load(xb[i][:], xsv[:, :, b * 512:(b + 1) * 512], Bxb[i], Axb[i], B_xdst_prev)
                        load(cb[i][:], rope_c[:, b * 512:(b + 1) * 512], Bcb[i], Acb[i])
                        load(sbt[i][:], rope_s[:, b * 512:(b + 1) * 512], Bcb[i], Acb[i])

                    issue_loads(0)
                    cnt = 0
                    for b in range(NB):
                        i = b % 2
                        if b + 1 < NB:
                            issue_loads(b + 1)
                        rmsnorm_T(xb[i], 512, l, 0, hT, sq, ssp, rstd, [Bxb[i]], Bh, Bsq, Bss, Brs, D)
                        chk('A1')
                        for fc in range(16):
                            j = cnt % 2
                            cnt += 1

                            def mma():
                                for kc in range(KC):
                                    ins = pe.matmul(pa[j][:], lhsT=Wqk[:, kc, fc * 128:(fc + 1) * 128], rhs=hT[:, kc, :], start=(kc == 0), stop=(kc == KC - 1))
                                return ins
                            S.op("pe", mma, reads=Bh + BW, writes=[Bpa[j]])
                            chk('A2a')
                            S.op("act", lambda: act.activation(out=qbf[j][:], in_=pa[j][:], func=AF.Copy), reads=[Bpa[j]], writes=[Bqbf[j]])
                            chk('A2b')
                            S.op("pe", lambda: pe.matmul(pb[j][:], lhsT=pmat[:], rhs=qbf[j][:], start=True, stop=True), reads=[Bqbf[j]] + CONST, writes=[Bpb[j]])
                            chk('A2c')
                            S.op("dve", lambda: dve.tensor_tensor(out=t1[j][:], in0=pa[j][:], in1=cb[i][:], op=ALU.mult), reads=[Bpa[j], Bcb[i]], writes=[Bt1[j]])
                            chk('A2t1')
                            S.op("dve", lambda: dve.tensor_tensor(out=t2[j][:], in0=pb[j][:], in1=sbt[i][:], op=ALU.mult), reads=[Bpb[j], Bcb[i]], writes=[Bt2[j]])
                            chk('A2d')
                            S.op("pool", lambda: pool.tensor_tensor(out=qko[:, fc, :], in0=t1[j][:], in1=t2[j][:], op=ALU.add), reads=[Bt1[j], Bt2[j]], writes=[Bqko])
                        chk('A2')
                        store(qkv_[:, :, b * 512:(b + 1) * 512], qko[:], Bqko, Aqko, [B_qk[b]])
                        chk('A3')
                        for ts in range(4):
                            for hf in range(2):
                                j = cnt % 2
                                cnt += 1

                                def mmv():
                                    for kc in range(KC):
                                        ins = pe.matmul(pv[j][:], lhsT=hT[:, kc, ts * 128:(ts + 1) * 128], rhs=Wv[:, kc, hf * 512:(hf + 1) * 512], start=(kc == 0), stop=(kc == KC - 1))
                                    return ins
                                S.op("pe", mmv, reads=Bh + BW, writes=[Bpv[j]])
                                S.op("act", lambda: act.activation(out=vo[:, ts, hf * 512:(hf + 1) * 512], in_=pv[j][:], func=AF.Copy), reads=[Bpv[j]], writes=[Bvo])
                        store(vtv[:, b * 4:(b + 1) * 4, :], vo[:], Bvo, Avo, [B_v[b]])

                chk('A')
                S.full_barrier()
                S.next = 2
                with contextlib.ExitStack() as es:
                    KT = sb(es, "KT", [128, T], BF16)
                    Vh = sb(es, "Vh", [128, NKC, 128], BF16)
                    db1 = sb(es, "db1", [128, NB * NKC], F32)
                    BK, BV, Bdb = Buf(), Buf(), Buf()
                    AK, AV, Adb = S.dma_agent(), S.dma_agent(), S.dma_agent()
                    load(db1[:], dbias1_in, Bdb, Adb)
                    QT = [sb(es, "QT%d" % i, [128, 512], BF16) for i in range(2)]
                    BQ = [Buf(), Buf()]
                    AQ = [S.dma_agent(), S.dma_agent()]
                    pT = [sb(es, "pT%d" % i, [128, 2, 512], BF16) for i in range(2)]
                    BpT = [Buf(), Buf()]
                    sT = [ps(es, "sT%d" % i, [128, 2, 512]) for i in range(2)]
                    BsT = [PB(), PB()]
                    accO = ps(es, "accO", [128, 2, 512])
                    accS = ps(es, "accS", [128, 2, 512])
                    Bacc = PB()
                    rr = sb(es, "rr", [128, 2, 512], F32)
                    o1 = sb(es, "o1", [128, 512], F32)
                    o2 = sb(es, "o2", [128, 512], F32)
                    osq = sb(es, "osq", [128, 512], BF16)
                    rs = sb(es, "rs", [128, 512], F32)
                    ob = [sb(es, "ob%d" % i, [128, 512], BF16) for i in range(2)]
                    Bfin = Buf()
                    Bob = [Buf(), Buf()]
                    Aob = [S.dma_agent(), S.dma_agent()]
                    nstep = [0]
                    for h in range(4):
                        load(KT[:], qkT[512 + h * 128:512 + (h + 1) * 128, :], BK, AK, B_qk)
                        for vp in range(0, NKC, 16):
                            load(Vh[:, vp:vp + 16, :], vtv[:, vp:vp + 16, h * 128:(h + 1) * 128], BV, AV, B_v)
                        steps = [(qb, kc) for qb in range(NB) for kc in range(NKC)]
                        base = nstep[0]

                        def load_q(qb):
                            qi = qb % 2
                            load(QT[qi][:], qkT[h * 128:(h + 1) * 128, qb * 512:(qb + 1) * 512], BQ[qi], AQ[qi], [B_qk[qb]])

                        def emit_qk(n):
                            qb, kc = steps[n]
                            si = (base + n) % 2
                            qi = qb % 2
                            if kc == min(1, NKC - 1) and qb + 1 < NB:
                                load_q(qb + 1)

                            def qk():
                                pe.matmul(sT[si][:, 0, :], lhsT=KT[0:64, kc * 128:(kc + 1) * 128], rhs=QT[qi][0:64, :], start=True, stop=True)
                                return pe.matmul(sT[si][:, 1, :], lhsT=KT[64:128, kc * 128:(kc + 1) * 128], rhs=QT[qi][64:128, :], start=True, stop=True)
                            S.op("pe", qk, reads=[BK, BQ[qi]], writes=[BsT[si]])

                        load_q(0)
                        emit_qk(0)
                        for n, (qb, kc) in enumerate(steps):
                            si = (base + n) % 2
                            if n + 1 < len(steps):
                                emit_qk(n + 1)
                            bi = qb * NKC + kc
                            S.op("act", lambda: act.activation(out=pT[si][:], in_=sT[si][:], func=AF.Exp, bias=db1[:, bi:bi + 1], scale=0.125), reads=[BsT[si], Bdb], writes=[BpT[si]])

                            def pvm():
                                for c in range(2):
                                    pe.matmul(accO[:, c, :], lhsT=Vh[:, kc, :], rhs=pT[si][:, c, :], start=(kc == 0), stop=(kc == NKC - 1))
                                    ins = pe.matmul(accS[:, c, :], lhsT=ones[:], rhs=pT[si][:, c, :], start=(kc == 0), stop=(kc == NKC - 1))
                                return ins
                            S.op("pe", pvm, reads=[BV, BpT[si]] + CONST, writes=[Bacc])
                            if kc != NKC - 1:
                                continue
                            oi = qb % 2
                            S.op("dve", lambda: dve.reciprocal(out=rr[:], in_=accS[:]), reads=[Bacc, Bfin], writes=[Bfin])
                            S.op("dve", lambda: dve.tensor_tensor(out=o1[:], in0=accO[:, 0, :], in1=rr[:, 0, :], op=ALU.mult), reads=[Bacc, Bfin], writes=[Bfin])
                            S.op("dve", lambda: dve.tensor_tensor(out=o2[:], in0=accO[:, 1, :], in1=rr[:, 1, :], op=ALU.mult), reads=[Bacc, Bfin], writes=[Bfin])
                            S.op("dve", lambda: dve.scalar_tensor_tensor(out=o1[:], in0=o2[:], scalar=neglam[:, l:l + 1], in1=o1[:], op0=ALU.mult, op1=ALU.add),
                                 reads=[Bfin] + CONST, writes=[Bfin])
                            S.op("act", lambda: act.activation(out=osq[:], in_=o1[:], func=AF.Square), reads=[Bfin], writes=[Bfin])
                            S.op("pe", lambda: pe.matmul(accS[:, 0, :], lhsT=ones[:], rhs=osq[:], start=True, stop=True), reads=[Bfin] + CONST, writes=[Bacc])
                            S.op("act", lambda: act.activation(out=rs[:], in_=accS[:, 0, :], func=AF.Sqrt, bias=epst[:], scale=1.0 / 128), reads=[Bacc] + CONST, writes=[Bfin])
                            S.op("dve", lambda: dve.reciprocal(out=rs[:], in_=rs[:]), reads=[Bfin], writes=[Bfin])
                            S.op("dve", lambda: dve.scalar_tensor_tensor(out=ob[oi][:], in0=o1[:], scalar=subg[:, l:l + 1], in1=rs[:], op0=ALU.mult, op1=ALU.mult),
                                 reads=[Bfin] + CONST, writes=[Bob[oi]])
                            store(oT[h * 128:(h + 1) * 128, qb * 512:(qb + 1) * 512], ob[oi][:], Bob[oi], Aob[oi], [B_oT[qb]])
                        nstep[0] += len(steps)

                chk('C1')
                S.full_barrier()
                S.next = 2
                with contextlib.ExitStack() as es:
                    KT = sb(es, "KTd", [128, T], BF16)
                    Vh = sb(es, "Vhd", [128, NKC, 128], BF16)
                    masks = sb(es, "masks", [128, 20, 512], BF16)
                    BK, BV, BM = Buf(), Buf(), Buf()
                    AK, AV = S.dma_agent(), S.dma_agent()
                    with contextlib.ExitStack() as es2:
                        stg = [sb(es2, "mstg%d" % i, [128, 2048], F32) for i in range(2)]
                        stgB = [Buf(), Buf()]
                        stgA = [S.dma_agent(), S.dma_agent()]
                        for g in range(5):
                            i = g % 2
                            load(stg[i][:], masks_in[:, g * 2048:(g + 1) * 2048], stgB[i], stgA[i])
                            S.op("pool", lambda: pool.tensor_copy(out=masks[:, g * 4:(g + 1) * 4, :], in_=stg[i][:].rearrange("p (a b) -> p a b", a=4)), reads=[stgB[i]], writes=[BM])
                    S.full_barrier()
                    QT = [sb(es, "QTd%d" % i, [128, 512], BF16) for i in range(2)]
                    BQ = [Buf(), Buf()]
                    AQ = [S.dma_agent(), S.dma_agent()]
                    eT = [sb(es, "eT%d" % i, [128, 2, 512], BF16) for i in range(2)]
                    emT = [sb(es, "emT%d" % i, [128, 2, 512], BF16) for i in range(2)]
                    BeT = [Buf(), Buf()]
                    BemT = [Buf(), Buf()]
                    sT = [ps(es, "sTd%d" % i, [128, 2, 512]) for i in range(2)]
                    BsT = [PB(), PB()]
                    accO = [ps(es, "accOd%d" % i, [128, 512]) for i in range(2)]
                    accS = [ps(es, "accSd%d" % i, [128, 512]) for i in range(2)]
                    Bacc = [PB(), PB()]
                    rr = [sb(es, "rrd%d" % i, [128, 512], F32) for i in range(2)]
                    Brr = [Buf(), Buf()]
                    ob = [sb(es, "obd%d" % i, [128, 512], BF16) for i in range(2)]
                    Bob = [Buf(), Buf()]
                    Aob = [S.dma_agent(), S.dma_agent()]
                    nstep = [0]
                    gcnt = [0]
                    for hp in range(4):
                        load(KT[:], qkT[1536 + hp * 128:1536 + (hp + 1) * 128, :], BK, AK, B_qk)
                        for vp in range(0, NKC, 16):
                            load(Vh[:, vp:vp + 16, :], vtv[:, vp:vp + 16, 512 + hp * 128:512 + (hp + 1) * 128], BV, AV, B_v)
                        steps = []
                        for qb in range(NB):
                            dl = [d_ for d_ in range(-8, 12) if 0 <= 4 * qb + d_ < NKC]
                            g_ = gcnt[0]
                            gcnt[0] += 1
                            for n_, d_ in enumerate(dl):
                                steps.append((qb, d_, n_ == 0, n_ == len(dl) - 1, g_ % 2))
                        base = nstep[0]

                        def load_q(qb):
                            qi = qb % 2
                            load(QT[qi][:], qkT[1024 + hp * 128:1024 + (hp + 1) * 128, qb * 512:(qb + 1) * 512], BQ[qi], AQ[qi], [B_qk[qb]])

                        def emit_qk(n):
                            qb, d_, first, last, ai = steps[n]
                            kc = 4 * qb + d_
                            si = (base + n) % 2
                            qi = qb % 2
                            if first and qb + 1 < NB:
                                load_q(qb + 1)

                            def qk():
                                pe.matmul(sT[si][:, 0, :], lhsT=KT[0:64, kc * 128:(kc + 1) * 128], rhs=QT[qi][0:64, :], start=True, stop=True)
                                return pe.matmul(sT[si][:, 1, :], lhsT=KT[64:128, kc * 128:(kc + 1) * 128], rhs=QT[qi][64:128, :], start=True, stop=True)
                            S.op("pe", qk, reads=[BK, BQ[qi]], writes=[BsT[si]])

                        load_q(0)
                        for n in range(min(2, len(steps))):
                            emit_qk(n)
                        for n, (qb, d_, first, last, ai) in enumerate(steps):
                            kc = 4 * qb + d_
                            si = (base + n) % 2
                            bi = qb * 20 + d_ + 8
                            S.op("act", lambda: act.activation(out=eT[si][:], in_=sT[si][:], func=AF.Exp, bias=dbias[:, bi:bi + 1], scale=0.125),
                                 reads=[BsT[si]] + CONST, writes=[BeT[si]])
                            if n + 2 < len(steps):
                                emit_qk(n + 2)

                            def mk():
                                dve.tensor_tensor(out=emT[si][:, 0, :], in0=eT[si][:, 0, :], in1=masks[:, d_ + 8, :], op=ALU.mult)
                                return dve.tensor_tensor(out=emT[si][:, 1, :], in0=eT[si][:, 1, :], in1=masks[:, d_ + 8, :], op=ALU.mult)
                            S.op("dve", mk, reads=[BeT[si], BM], writes=[BemT[si]])

                            def pvm():
                                for hh in range(2):
                                    pe.matmul(accO[ai][hh * 64:(hh + 1) * 64, :], lhsT=Vh[:, kc, hh * 64:(hh + 1) * 64], rhs=emT[si][:, hh, :], start=first, stop=last)
                                for hh in range(2):
                                    ins = pe.matmul(accS[ai][hh * 64:(hh + 1) * 64, :], lhsT=ones[:, hh * 64:(hh + 1) * 64], rhs=emT[si][:, hh, :], start=first, stop=last)
                                return ins
                            S.op("pe", pvm, reads=[BV, BemT[si]] + CONST, writes=[Bacc[ai]])
                            if not last:
                                continue
                            S.op("dve", lambda: dve.reciprocal(out=rr[ai][:], in_=accS[ai][:]), reads=[Bacc[ai]], writes=[Brr[ai]])
                            S.op("dve", lambda: dve.tensor_tensor(out=ob[ai][:], in0=accO[ai][:], in1=rr[ai][:], op=ALU.mult), reads=[Bacc[ai], Brr[ai]], writes=[Bob[ai]])
                            r0 = 512 + hp * 128
                            store(oT[r0:r0 + 128, qb * 512:(qb + 1) * 512], ob[ai][:], Bob[ai], Aob[ai], [B_oT[qb]])
                        nstep[0] += len(steps)

                chk('C2')
                S.full_barrier()
                S.next = 2
                with contextlib.ExitStack() as es:
                    Kmem = sb(es, "Kmem", [128, 4, 8, M], BF16)
                    Vmem = sb(es, "Vmem", [128, 4, 2, D], BF16)
                    BKV = Buf()
                    with contextlib.ExitStack() as es2:
                        Wkv = sb(es2, "Wkv", [128, KC, 2 * D], BF16)
                        stg = [sb(es2, "kstg%d" % i, [128, 2048], F32) for i in range(2)]
                        stgB = [Buf(), Buf()]
                        stgA = [S.dma_agent(), S.dma_agent()]
                        BWkv = [Buf(), Buf()]
                        load_weight(Wkv, w_xkv[l], D, 2 * D, BWkv, stg, stgB, stgA, 2048)
                        mb = sb(es2, "mb", [128, KC, M], F32)
                        mh = sb(es2, "mh", [128, KC, M], BF16)
                        sq = sb(es2, "ksq", [128, KC, M], BF16)
                        rstd = sb(es2, "krstd", [128, M], F32)
                        ssp = ps(es2, "kssp", [128, 512])
                        pm = [ps(es2, "kpm%d" % i, [128, 512]) for i in range(2)]
                        Bpm = [PB(), PB()]
                        Bmb, Bsq, Bss, Brs = Buf(), Buf(), PB(), Buf()
                        Bmh = [Buf(), Buf()]
                        Amb = S.dma_agent()
                        cnt = 0
                        for s_ in range(4):
                            load(mb[:], memT_in[s_].rearrange("(kc p) m -> p kc m", p=128), Bmb, Amb)
                            rmsnorm_T(mb, M, l, 2, mh, sq, ssp, rstd, [Bmb], Bmh, Bsq, Bss, Brs, D)
                            for fc in range(8):
                                j = cnt % 2
                                cnt += 1

                                def mmk():
                                    for kc in range(KC):
                                        ins = pe.matmul(pm[j][:, 0:M], lhsT=Wkv[:, kc, fc * 128:(fc + 1) * 128], rhs=mh[:, kc, :], start=(kc == 0), stop=(kc == KC - 1))
                                    return ins
                                S.op("pe", mmk, reads=Bmh + BWkv, writes=[Bpm[j]])
                                S.op("act", lambda: act.activation(out=Kmem[:, s_, fc, :], in_=pm[j][:, 0:M], func=AF.Copy), reads=[Bpm[j]], writes=[BKV])
                            for mc in range(2):
                                for hf in range(2):
                                    j = cnt % 2
                                    cnt += 1

                                    def mmvv():
                                        for kc in range(KC):
                                            ins = pe.matmul(pm[j][:], lhsT=mh[:, kc, mc * 128:(mc + 1) * 128], rhs=Wkv[:, kc, D + hf * 512:D + (hf + 1) * 512], start=(kc == 0), stop=(kc == KC - 1))
                                        return ins
                                    S.op("pe", mmvv, reads=Bmh + BWkv, writes=[Bpm[j]])
                                    S.op("act", lambda: act.activation(out=Vmem[:, s_, mc, hf * 512:(hf + 1) * 512], in_=pm[j][:], func=AF.Copy), reads=[Bpm[j]], writes=[BKV])

                    S.full_barrier()
                    Wo = sb(es, "Wo", [128, KC, D], BF16)
                    Wq = sb(es, "Wq", [128, KC, D], BF16)
                    Wx = sb(es, "Wx", [128, KC, D], BF16)
                    BW = [Buf(), Buf()]
                    stg = [sb(es, "dstg%d" % i, [128, 1024], F32) for i in range(2)]
                    stgB = [Buf(), Buf()]
                    stgA = [S.dma_agent(), S.dma_agent()]
                    load_weight(Wo, w_out[l], D, D, BW, stg, stgB, stgA, 1024)
                    load_weight(Wq, w_xq[l], D, D, BW, stg, stgB, stgA, 1024)
                    load_weight(Wx, w_xo[l], D, D, BW, stg, stgB, stgA, 1024)
                    xb = sb(es, "dxb", [128, KC, 512], F32)
                    ob_ = sb(es, "dob", [128, KC, 512], BF16)
                    Bxb, Bob_ = Buf(), Buf()
                    Axb, Aob_, Axs = S.dma_agent(), S.dma_agent(), S.dma_agent()
                    sqox = sb(es, "dsqox", [128, KC, 512], BF16)
                    rstd = sb(es, "drstd", [128, 512], F32)
                    hT = sb(es, "dhT", [128, KC, 512], BF16)
                    mf = sb(es, "mf", [128, KC, 512], F32)
                    tmp = [sb(es, "dtmp%d" % i, [128, 512], F32) for i in range(2)]
                    qx = sb(es, "qx", [128, KC, 512], BF16)
                    pxx = sb(es, "pxx", [128, 2, 512], BF16)
                    rx = sb(es, "rx", [128, 512], F32)
                    Ah3 = S.dma_agent()
                    Bsqox, Brs, Bmf, Bqx, Bpx, Brx = Buf(), Buf(), Buf(), Buf(), Buf(), Buf()
                    Bh = [Buf(), Buf()]
                    Btmp = [Buf(), Buf()]
                    ssp = ps(es, "dssp", [128, 512])
                    pm = [ps(es, "pm%d" % i, [128, 512]) for i in range(2)]
                    psx = [ps(es, "psx%d" % i, [128, 512]) for i in range(2)]
                    pox = [ps(es, "pox%d" % i, [128, 512]) for i in range(2)]
                    pls = ps(es, "pls", [128, 512])
                    Bss, Bls = PB(), PB()
                    Bpm = [PB(), PB()]
                    Bpsx = [PB(), PB()]
                    Bpox = [PB(), PB()]

                    def proj_post_add(Wt, src, Bsrc, gwhich):
                        for n_ in range(KC):
                            j = n_ % 2

                            def mm():
                                for kc in range(KC):
                                    ins = pe.matmul(pm[j][:], lhsT=Wt[:, kc, n_ * 128:(n_ + 1) * 128], rhs=src[:, kc, :], start=(kc == 0), stop=(kc == KC - 1))
                                return ins
                            S.op("pe", mm, reads=Bsrc + BW, writes=[Bpm[j]])
                            S.op("act", lambda: act.activation(out=mf[:, n_, :], in_=pm[j][:], func=AF.Copy), reads=[Bpm[j]], writes=[Bmf])
                        rmsnorm_T(mf, 512, l, 0, None, sqox, ssp, rstd, [Bmf], None, Bsqox, Bss, Brs, D)
                        for n_ in range(KC):
                            j = n_ % 2
                            S.op("dve", lambda: dve.scalar_tensor_tensor(out=tmp[j][:], in0=mf[:, n_, :], scalar=gcol(l, gwhich, n_), in1=rstd[:], op0=ALU.mult, op1=ALU.mult),
                                 reads=[Bmf, Brs] + CONST, writes=[Btmp[j]])
                            S.op("pool", lambda: pool.tensor_tensor(out=xb[:, n_, :], in0=xb[:, n_, :], in1=tmp[j][:], op=ALU.add), reads=[Btmp[j], Bxb], writes=[Bxb])

                    for b in range(NB):
                        s_ = (b * 512) // SEG
                        load(xb[:], xsv[:, :, b * 512:(b + 1) * 512], Bxb, Axb, B_xdst_prev)
                        load(ob_[:], oTv[:, :, b * 512:(b + 1) * 512], Bob_, Aob_, [B_oT[b]])
                        proj_post_add(Wo, ob_, [Bob_], 1)
                        rmsnorm_T(xb, 512, l, 3, hT, sqox, ssp, rstd, [Bxb], Bh, Bsqox, Bss, Brs, D)
                        for fc in range(KC):
                            j = fc % 2

                            def mmq():
                                for kc in range(KC):
                                    ins = pe.matmul(pm[j][:], lhsT=Wq[:, kc, fc * 128:(fc + 1) * 128], rhs=hT[:, kc, :], start=(kc == 0), stop=(kc == KC - 1))
                                return ins
                            S.op("pe", mmq, reads=Bh + BW, writes=[Bpm[j]])
                            S.op("act", lambda: act.activation(out=qx[:, fc, :], in_=pm[j][:], func=AF.Copy), reads=[Bpm[j]], writes=[Bqx])
                        for hx in range(4):
                            for mc in range(2):
                                def mms():
                                    for dc in range(2):
                                        ins = pe.matmul(psx[mc][:], lhsT=Kmem[:, s_, hx * 2 + dc, mc * 128:(mc + 1) * 128], rhs=qx[:, hx * 2 + dc, :], start=(dc == 0), stop=(dc == 1))
                                    return ins
                                S.op("pe", mms, reads=[Bqx, BKV], writes=[Bpsx[mc]])
                                S.op("act", lambda: act.activation(out=pxx[:, mc, :], in_=psx[mc][:], func=AF.Exp, scale=1.0 / 16.0), reads=[Bpsx[mc]], writes=[Bpx])

                            def mml():
                                for mc in range(2):
                                    ins = pe.matmul(pls[:], lhsT=ones[:], rhs=pxx[:, mc, :], start=(mc == 0), stop=(mc == 1))
                                return ins
                            S.op("pe", mml, reads=[Bpx] + CONST, writes=[Bls])
                            S.op("act", lambda: act.activation(out=rx[:], in_=pls[:], func=AF.Ln), reads=[Bls], writes=[Brx])
                            S.op("act", lambda: act.activation(out=rx[:], in_=rx[:], func=AF.Exp, scale=-1.0), reads=[Brx], writes=[Brx])
                            for dc in range(2):
                                def mmo():
                                    for mc in range(2):
                                        ins = pe.matmul(pox[dc][:], lhsT=Vmem[:, s_, mc, hx * 256 + dc * 128:hx * 256 + (dc + 1) * 128], rhs=pxx[:, mc, :], start=(mc == 0), stop=(mc == 1))
                                    return ins
                                S.op("pe", mmo, reads=[Bpx, BKV], writes=[Bpox[dc]])
                                S.op("dve", lambda: dve.tensor_tensor(out=sqox[:, hx * 2 + dc, :], in0=pox[dc][:], in1=rx[:], op=ALU.mult), reads=[Bpox[dc], Brx], writes=[Bsqox])
                        proj_post_add(Wx, sqox, [Bsqox], 4)
                        rmsnorm_T(xb, 512, l, 5, hT, sqox, ssp, rstd, [Bxb], Bh, Bsqox, Bss, Brs, D)
                        store(h3v[:, :, PAD + b * 512:PAD + (b + 1) * 512], hT[:], Bh[0], Ah3, [B_h3[b]])
                        S.barrier("sp", [Bh[1]])
                        store(xmv[:, :, b * 512:(b + 1) * 512], xb[:], Bxb, Axs, [B_xmid[b]])

                chk('D12')
                S.full_barrier()
                S.next = 2
                with contextlib.ExitStack() as es:
                    Wd = sb(es, "Wd", [128, NPAIR, D], BF16)
                    BWd = [Buf(), Buf()]
                    stg = [sb(es, "fstg%d" % i, [128, 1024], F32) for i in range(2)]
                    stgB = [Buf(), Buf()]
                    stgA = [S.dma_agent(), S.dma_agent()]
                    load_weight(Wd, w_down[l], DFF, D, BWd, stg, stgB, stgA, 1024)
                    cst = [sb(es, "cst%d" % i, [128, 1024], BF16) for i in range(2)]
                    cstB = [Buf(), Buf()]
                    cstA = [S.dma_agent(), S.dma_agent()]
                    wuv = w_up[l].rearrange("(kc p) n -> p kc n", p=128)
                    wubv = wupb.rearrange("(kc p) n -> p kc n", p=128)
                    cc = 0
                    for kc in range(KC):
                        for c0 in range(0, NUP, 1024):
                            cw = min(1024, NUP - c0)
                            i = cc % 2
                            cc += 1
                            load(stg[i][:, 0:cw], wuv[:, kc, c0:c0 + cw], stgB[i], stgA[i])
                            eng = ["pool", "dve"][i]
                            E = S.E[eng]
                            S.op(eng, lambda: E.tensor_copy(out=cst[i][:, 0:cw], in_=stg[i][:, 0:cw]), reads=[stgB[i]], writes=[cstB[i]])
                            store(wubv[:, kc, c0:c0 + cw], cst[i][:, 0:cw], cstB[i], cstA[i], [B_wup])
                    wu = [sb(es, "wu%d" % i, [128, KC, 2, 256], BF16) for i in range(2)]
                    Bwu = [Buf(), Buf()]
                    Awu = [S.dma_agent(), S.dma_agent()]
                    hb = sb(es, "hb", [128, KC, 512], BF16)
                    Bhb = Buf()
                    Ahb = S.dma_agent()
                    xb = sb(es, "fxb", [128, KC, 512], F32)
                    Bxb = Buf()
                    Axb, Axs = S.dma_agent(), S.dma_agent()
                    actT = sb(es, "actT", [128, NPAIR, 512], BF16)
                    Bact = Buf()
                    cg = [sb(es, "cg%d" % i, [128, 512], F32) for i in range(2)]
                    cv = [sb(es, "cv%d" % i, [128, 512], F32) for i in range(2)]
                    gg = [sb(es, "gg%d" % i, [128, 512], F32) for i in range(2)]
                    Bcg = [Buf(), Buf()]
                    Bcv = [Buf(), Buf()]
                    Bgg = [Buf(), Buf()]
                    fx = sb(es, "fx", [128, 1], F32)
                    Bfx = Buf()
                    mf = sb(es, "fmf", [128, KC, 512], F32)
                    rstd = sb(es, "frstd", [128, 512], F32)
                    tmp = [sb(es, "ftmp%d" % i, [128, 512], F32) for i in range(2)]
                    Bmf, Brs = Buf(), Buf()
                    Btmp = [Buf(), Buf()]
                    pug = [ps(es, "pug%d" % i, [128, 512]) for i in range(2)]
                    puv = [ps(es, "puv%d" % i, [128, 512]) for i in range(2)]
                    pf = [ps(es, "pf%d" % i, [128, 512]) for i in range(2)]
                    cgp = ps(es, "cgp", [128, 512])
                    cvp = ps(es, "cvp", [128, 512])
                    Bpug = [PB(), PB()]
                    Bpuv = [PB(), PB()]
                    Bpf = [PB(), PB()]
                    Bcgp, Bcvp = PB(), PB()
                    ubg = [sb(es, "ubg%d" % i, [128, 512], BF16) for i in range(2)]
                    ubv = [sb(es, "ubv%d" % i, [128, 512], BF16) for i in range(2)]
                    Bubg = [Buf(), Buf()]
                    Bubv = [Buf(), Buf()]
                    dgt = [sb(es, "dgt%d" % i, [128, 6, 128], BF16) for i in range(2)]
                    Bdg = [Buf(), Buf()]
                    ssp = pug[0]
                    Bss = Bpug[0]

                    def cw_(t, f):
                        i = (l * 3 + t) * 44 + f
                        return convw[:, i:i + 1]

                    def cb_(f):
                        i = l * 44 + f
                        return convb[:, i:i + 1]

                    def wu_load(gi, slot):
                        load(wu[slot][:, :, 0, :], wubv[:, :, gi * 256:(gi + 1) * 256], Bwu[slot], Awu[slot], [B_wup])
                        load(wu[slot][:, :, 1, :], wubv[:, :, DFF + gi * 256:DFF + (gi + 1) * 256], Bwu[slot], Awu[slot], [B_wup])

                    wu_load(0, 0)
                    for b in range(NFB):
                        nout = min(FW, T - FW * b)
                        c_lo = PAD - 1 + FW * b
                        load(hb[:], h3v[:, :, c_lo:c_lo + 512], Bhb, Ahb, B_h3 + [B_h3pad])
                        load(xb[:, :, 0:nout], xmv[:, :, FW * b:FW * b + nout], Bxb, Axb, B_xmid)
                        bounds = [SEG * k - FW * b for k in range(1, 4) if 0 <= SEG * k - FW * b <= FW]
                        wbase = b * 11

                        def emit_up(pj):
                            gi, jj = pj // 2, pj % 2
                            slot = (wbase + gi) % 2
                            p2 = pj % 2
                            if jj == 0:
                                if gi + 1 < 11:
                                    wu_load(gi + 1, (slot + 1) % 2)
                                elif b + 1 < NFB:
                                    wu_load(0, (slot + 1) % 2)

                            def mmu(dst, half):
                                for kc in range(KC):
                                    ins = pe.matmul(dst[:], lhsT=wu[slot][:, kc, half, jj * 128:(jj + 1) * 128], rhs=hb[:, kc, :], start=(kc == 0), stop=(kc == KC - 1))
                                return ins
                            S.op("pe", lambda: mmu(pug[p2], 0), reads=[Bwu[slot], Bhb], writes=[Bpug[p2]])
                            S.op("pe", lambda: mmu(puv[p2], 1), reads=[Bwu[slot], Bhb], writes=[Bpuv[p2]])
                            if not bounds:
                                S.op("act", lambda: act.activation(out=ubg[p2][:], in_=pug[p2][:], func=AF.Copy), reads=[Bpug[p2]], writes=[Bubg[p2]])
                                S.op("act", lambda: act.activation(out=ubv[p2][:], in_=puv[p2][:], func=AF.Copy), reads=[Bpuv[p2]], writes=[Bubv[p2]])

                        def emit_conv_pe(pj):
                            k_ = pj % 2
                            def mkdiag():
                                for hf, fo in ((0, pj), (1, NPAIR + pj)):
                                    for t in range(3):
                                        ins = dve.tensor_scalar(out=dgt[k_][:, hf * 3 + t, :], in0=identb[:], scalar1=cw_(t, fo), scalar2=None, op0=ALU.mult)
                                return ins
                            S.op("dve", mkdiag, reads=CONST, writes=[Bdg[k_]])
                            for hf, (dst, Bdst, ub, Bub) in enumerate(((cgp, Bcgp, ubg[k_], Bubg[k_]), (cvp, Bcvp, ubv[k_], Bubv[k_]))):
                                def mmc():
                                    for t in range(3):
                                        ins = pe.matmul(dst[:, 0:FW], lhsT=dgt[k_][:, hf * 3 + t, :], rhs=ub[:, t:t + FW], start=(t == 0), stop=(t == 2))
                                    return ins
                                S.op("pe", mmc, reads=[Bdg[k_], Bub], writes=[Bdst])
                            S.op("act", lambda: act.activation(out=gg[k_][:, 0:FW], in_=cgp[:, 0:FW], func=AF.Gelu_apprx_tanh, bias=cb_(pj)), reads=[Bcgp] + CONST, writes=[Bgg[k_]])
                            S.op("dve", lambda: dve.scalar_tensor_tensor(out=actT[:, pj, 0:FW], in0=cvp[:, 0:FW], scalar=cb_(NPAIR + pj), in1=gg[k_][:, 0:FW], op0=ALU.add, op1=ALU.mult),
                                 reads=[Bcvp, Bgg[k_]] + CONST, writes=[Bact])

                        def emit_conv_dve(pj):
                            k_ = pj % 2
                            p2 = pj % 2
                            for (pu, Bpu, cdst, Bc, fo) in ((pug[p2], Bpug[p2], cg[k_], Bcg[k_], pj), (puv[p2], Bpuv[p2], cv[k_], Bcv[k_], NPAIR + pj)):
                                S.op("act", lambda: act.activation(out=cdst[:, 0:FW], in_=pu[:, 1:1 + FW], func=AF.Identity, bias=cb_(fo), scale=cw_(1, fo)),
                                     reads=[Bpu] + CONST, writes=[Bc])
                                S.op("dve", lambda: dve.scalar_tensor_tensor(out=cdst[:, 0:FW], in0=pu[:, 0:FW], scalar=cw_(0, fo), in1=cdst[:, 0:FW], op0=ALU.mult, op1=ALU.add),
                                     reads=[Bpu, Bc] + CONST, writes=[Bc])
                                S.op("dve", lambda: dve.scalar_tensor_tensor(out=cdst[:, 0:FW], in0=pu[:, 2:2 + FW], scalar=cw_(2, fo), in1=cdst[:, 0:FW], op0=ALU.mult, op1=ALU.add),
                                     reads=[Bpu, Bc] + CONST, writes=[Bc])
                                for jb in bounds:
                                    if jb < FW:
                                        S.op("dve", lambda: dve.tensor_scalar(out=fx[:], in0=pu[:, jb:jb + 1], scalar1=cw_(0, fo), scalar2=negflag[:, 0:1], op0=ALU.mult, op1=ALU.mult),
                                             reads=[Bpu, Bfx] + CONST, writes=[Bfx])
                                        S.op("dve", lambda: dve.tensor_tensor(out=cdst[:, jb:jb + 1], in0=cdst[:, jb:jb + 1], in1=fx[:], op=ALU.add), reads=[Bfx, Bc], writes=[Bc])
                                    if jb >= 1:
                                        S.op("dve", lambda: dve.tensor_scalar(out=fx[:], in0=pu[:, jb + 1:jb + 2], scalar1=cw_(2, fo), scalar2=negflag[:, 0:1], op0=ALU.mult, op1=ALU.mult),
                                             reads=[Bpu, Bfx] + CONST, writes=[Bfx])
                                        S.op("dve", lambda: dve.tensor_tensor(out=cdst[:, jb - 1:jb], in0=cdst[:, jb - 1:jb], in1=fx[:], op=ALU.add), reads=[Bfx, Bc], writes=[Bc])
                            S.op("act", lambda: act.activation(out=gg[k_][:, 0:FW], in_=cg[k_][:, 0:FW], func=AF.Gelu_apprx_tanh), reads=[Bcg[k_]], writes=[Bgg[k_]])
                            S.op("pool", lambda: pool.tensor_tensor(out=actT[:, pj, 0:FW], in0=gg[k_][:, 0:FW], in1=cv[k_][:, 0:FW], op=ALU.mult), reads=[Bgg[k_], Bcv[k_]], writes=[Bact])

                        emit_up(0)
                        for pj in range(NPAIR):
                            if pj + 1 < NPAIR:
                                emit_up(pj + 1)
                            if bounds:
                                emit_conv_dve(pj)
                            else:
                                emit_conv_pe(pj)
                        for n_ in range(KC):
                            j = n_ % 2

                            def mmd():
                                for pj in range(NPAIR):
                                    ins = pe.matmul(pf[j][:, 0:FW], lhsT=Wd[:, pj, n_ * 128:(n_ + 1) * 128], rhs=actT[:, pj, 0:FW], start=(pj == 0), stop=(pj == NPAIR - 1))
                                return ins
                            S.op("pe", mmd, reads=[Bact] + BWd, writes=[Bpf[j]])
                            S.op("act", lambda: act.activation(out=mf[:, n_, 0:FW], in_=pf[j][:, 0:FW], func=AF.Copy), reads=[Bpf[j]], writes=[Bmf])
                        rmsnorm_T(mf, FW, l, 0, None, actT[:, 0:KC, :], ssp, rstd, [Bmf], None, Bact, Bss, Brs, D)
                        for n_ in range(KC):
                            j = n_ % 2
                            S.op("dve", lambda: dve.scalar_tensor_tensor(out=tmp[j][:, 0:FW], in0=mf[:, n_, 0:FW], scalar=gcol(l, 6, n_), in1=rstd[:, 0:FW], op0=ALU.mult, op1=ALU.mult),
                                 reads=[Bmf, Brs] + CONST, writes=[Btmp[j]])
                            S.op("pool", lambda: pool.tensor_tensor(out=xb[:, n_, 0:nout], in0=xb[:, n_, 0:nout], in1=tmp[j][:, 0:nout], op=ALU.add), reads=[Btmp[j], Bxb], writes=[Bxb])
                        store(xdv[:, :, FW * b:FW * b + nout], xb[:, :, 0:nout], Bxb, Axs, [B_xdst])
                B_xdst_prev = [B_xdst]

        except StopBuild:
            pass
        S.barrier("sp", B_xdst_prev)
    return nc


def _pmat():
    m = np.arange(128)
    pm = np.zeros((128, 128), np.float32)
    pm[(m // 64) * 64 + (m % 64 + 32) % 64, m] = 1.0
    return pm


def _rope_tables(pos):
    inv = (1.0 / (10000.0 ** (np.arange(0, 64, 2, dtype=np.float32) / np.float32(64)))).astype(np.float32)
    ang = (pos.astype(np.float32)[:, None] * inv[None, :]).astype(np.float32)
    c = np.cos(ang).astype(np.float32)
    s = np.sin(ang).astype(np.float32)
    C = np.concatenate([c, c, c, c], axis=1).T
    Sg = np.concatenate([-s, s, -s, s], axis=1).T
    return np.ascontiguousarray(C), np.ascontiguousarray(Sg)


def _dil_masks():
    kk = np.arange(128)[:, None]
    qq = np.arange(512)[None, :]
    out = np.zeros((128, 20, 512), np.float32)
    for di, d_ in enumerate(range(-8, 12)):
        delta = d_ * 128 + kk - qq
        m = (np.abs(delta) <= 64).astype(np.float32)
        m += ((np.abs(delta) <= 256) & (delta % 4 == 0)).astype(np.float32)
        m += ((np.abs(delta) <= 1024) & (delta % 16 == 0)).astype(np.float32)
        out[:, di, :] = m
    return out.reshape(128, 20 * 512)


def _prep_shared(w_in, w_out, lambda_q1, lambda_k1, lambda_q2, lambda_k2, diff_subln_g, w_xq, w_xkv, w_xo, w_up,
                 conv_w, conv_b, w_down, mix_pre_g, mix_post_g, mem_norm_g, xattn_pre_g, xattn_post_g, ffn_pre_g, ffn_post_g):
    f = np.float32
    qk_cols = np.concatenate([np.arange(0, 1024), np.arange(1536, 2560)])
    v_cols = np.concatenate([np.arange(1024, 1536), np.arange(2560, 3072)])
    w_qk = np.ascontiguousarray(w_in[:, :, qk_cols], dtype=f)
    w_v = np.ascontiguousarray(w_in[:, :, v_cols], dtype=f)
    gl = [mix_pre_g, mix_post_g, mem_norm_g, xattn_pre_g, xattn_post_g, ffn_pre_g, ffn_post_g]
    gains = np.zeros((128, L * 7 * 8), f)
    for l in range(L):
        for w_, g in enumerate(gl):
            gains[:, (l * 7 + w_) * 8:(l * 7 + w_ + 1) * 8] = np.asarray(g[l], f).reshape(8, 128).T
    convw = np.zeros((128, L * 3 * 44), f)
    convb = np.zeros((128, L * 44), f)
    for l in range(L):
        for t in range(3):
            convw[:, (l * 3 + t) * 44:(l * 3 + t + 1) * 44] = np.asarray(conv_w[l, t], f).reshape(44, 128).T
        convb[:, l * 44:(l + 1) * 44] = np.asarray(conv_b[l], f).reshape(44, 128).T
    subg = np.ascontiguousarray(np.asarray(diff_subln_g, f).T)
    lamv = np.zeros((128, L * 4 * 64), f)
    for l in range(L):
        for j, v in enumerate([lambda_q1, lambda_k1, lambda_q2, lambda_k2]):
            lamv[:, (l * 4 + j) * 64:(l * 4 + j + 1) * 64] = np.broadcast_to(np.asarray(v[l], f)[None, :], (128, 64))
    return dict(w_qk=w_qk, w_v=w_v, pmat=_pmat(), ident=np.eye(128, dtype=np.float32), w_out=np.ascontiguousarray(w_out, f), w_xq=np.ascontiguousarray(w_xq, f),
                w_xkv=np.ascontiguousarray(w_xkv, f), w_xo=np.ascontiguousarray(w_xo, f), w_up=np.ascontiguousarray(w_up, f),
                w_down=np.ascontiguousarray(w_down, f), gains=gains, convw=convw, convb=convb, subg=subg, lamv=lamv,
                masks=_dil_masks())


def _unit_inputs(T, xs, mems, packed):
    f = np.float32
    NB = T // 512
    SEG = T // 4
    x = np.concatenate(xs, axis=0)
    assert x.shape[0] == T
    xT = np.ascontiguousarray(x.T, dtype=f)
    memT = np.ascontiguousarray(np.stack([m.T for m in mems], axis=0), dtype=f)
    if packed:
        pos = np.arange(T) % SEG
    else:
        pos = np.arange(T)
    C, Sg = _rope_tables(pos)
    dbias = np.zeros((128, NB * 20), f)
    if packed:
        for qb in range(NB):
            for di, d_ in enumerate(range(-8, 12)):
                kc = 4 * qb + d_
                if 0 <= kc < T // 128 and (kc * 128) // SEG != (qb * 512) // SEG:
                    dbias[:, qb * 20 + di] = NEG
    dbias1 = np.zeros((128, NB * (T // 128)), f)
    if packed:
        for qb in range(NB):
            for kc in range(T // 128):
                if (kc * 128) // SEG != (qb * 512) // SEG:
                    dbias1[:, qb * (T // 128) + kc] = NEG
    segflag = np.full((128, 1), 1.0 if packed else 0.0, f)
    return dict(xT=xT, memT=memT, rope_c=C, rope_s=Sg, dbias=dbias, dbias1=dbias1, segflag=segflag)


_NC_CACHE = {}


def run_units(T, units, shared):
    if T not in _NC_CACHE:
        _NC_CACHE[T] = build(T)
    nc = _NC_CACHE[T]
    in_maps = []
    for u in units:
        m = dict(shared)
        m.update(u)
        in_maps.append(m)
    res = run_bass_kernel_spmd(nc, in_maps, core_ids=list(range(len(in_maps))))
    return [np.asarray(r["yT"]) for r in res.results]


def kernel(x_prompt, x_sample, mem_prompt, mem_sample, **params):
    T = 16384
    x_prompt = np.asarray(x_prompt, np.float32)
    x_sample = np.asarray(x_sample, np.float32)
    mem_prompt = np.asarray(mem_prompt, np.float32)
    mem_sample = np.asarray(mem_sample, np.float32)
    shared = _prep_shared(**{k: np.asarray(v, np.float32) for k, v in params.items()})
    u_s0 = _unit_inputs(T, [x_sample[0]], [mem_sample[0]] * 4, False)
    u_s1 = _unit_inputs(T, [x_sample[1]], [mem_sample[1]] * 4, False)
    u_p = _unit_inputs(T, [x_prompt[i] for i in range(4)], [mem_prompt[i] for i in range(4)], True)
    units = [u_s0, u_s1, u_p] + [u_p] * 5
    ys = run_units(T, units, shared)
    y_sample = np.stack([ys[0].T, ys[1].T], axis=0).astype(np.float32)
    yp = ys[2].T
    y_prompt = np.ascontiguousarray(yp.reshape(4, 4096, D)).astype(np.float32)
    return (y_prompt, np.ascontiguousarray(y_sample))
```

```python
import numpy as np
import concourse.bass as bass
import concourse.mybir as mybir
from concourse.bass_utils import run_bass_kernel_spmd

F32 = mybir.dt.float32
BF16 = mybir.dt.bfloat16
AF = mybir.ActivationFunctionType
ALU = mybir.AluOpType
AX = mybir.AxisListType

D = 1024
KC = 8
L = 2
NQK = 2048
NV = 1024
DFF = 2816
NUP = 5632
NPAIR = 22
M = 256
EPS = 1e-6
NEG = -30000.0
FW = 510
STOP = []


class StopBuild(Exception):
    pass


DEAD = [False]


def chk(tag):
    if STOP and STOP[0] == tag:
        DEAD[0] = True


class Agent:
    def __init__(self, sem, step):
        self.sem = sem
        self.step = step
        self.count = 0


class Buf:
    __slots__ = ("w", "r", "x")

    def __init__(self, x=False):
        self.w = {}
        self.r = {}
        self.x = x


def PB():
    return Buf(True)


class Sched:
    def __init__(self, nc, sems, dma_sems):
        self.nc = nc
        self.E = {"pe": nc.tensor, "act": nc.scalar, "dve": nc.vector, "pool": nc.gpsimd, "sp": nc.sync}
        self.ag = {k: Agent(sems[k], 1) for k in self.E}
        self.seen = {k: {} for k in self.E}
        self.pool = [Agent(x, 16) for x in dma_sems]
        self.next = 0

    def dma_agent(self):
        a = self.pool[self.next]
        self.next += 1
        return a

    def op(self, eng, fn, reads=(), writes=(), agent=None):
        if DEAD[0]:
            return None
        ag = agent if agent is not None else self.ag[eng]
        deps = {}

        def add(a, c):
            if deps.get(a, 0) < c:
                deps[a] = c

        for b in reads:
            for a, c in b.w.items():
                add(a, c)
            if b.x:
                for a, c in b.r.items():
                    if a is not ag:
                        add(a, c)
        for b in writes:
            for a, c in b.w.items():
                add(a, c)
            for a, c in b.r.items():
                add(a, c)
        if agent is not None and ag.count > 0:
            add(ag, ag.count)
        seen = self.seen[eng]
        e = self.E[eng]
        for a, c in deps.items():
            if agent is None and a is ag and eng == "pe":
                continue
            if seen.get(a, 0) >= c:
                continue
            e.wait_ge(a.sem, c)
            seen[a] = c
        ins = fn()
        ag.count += ag.step
        ins.then_inc(ag.sem, ag.step)
        for b in reads:
            b.r[ag] = ag.count
        for b in writes:
            b.w[ag] = ag.count
            b.r = {}
        return ins

    def full_barrier(self):
        if DEAD[0]:
            return
        agents = list(self.ag.values()) + [a for a in self.pool if a.count > 0]
        for eng, e in self.E.items():
            for a in agents:
                if a is self.ag[eng] or a.count == 0:
                    continue
                if self.seen[eng].get(a, 0) < a.count:
                    e.wait_ge(a.sem, a.count)
                    self.seen[eng][a] = a.count

    def barrier(self, eng, bufs):
        if DEAD[0]:
            return
        e = self.E[eng]
        for b in bufs:
            for a, c in b.w.items():
                if self.seen[eng].get(a, 0) < c:
                    e.wait_ge(a.sem, c)
                    self.seen[eng][a] = c


def build(T):
    NB = T // 512
    NKC = T // 128
    NFB = (T + FW - 1) // FW
    PAD = 64
    HW = ((PAD - 1 + FW * (NFB - 1) + 512 + 63) // 64) * 64
    SEG = T // 4
    assert SEG % 512 == 0
    nc = bass.Bass("TRN2", target_bir_lowering=False)

    def din(name, shape, dt=F32):
        return nc.dram_tensor(name, list(shape), dt, kind="ExternalInput").ap()

    def dscr(name, shape, dt):
        return nc.dram_tensor(name, list(shape), dt, kind="Internal").ap()

    xT_in = din("xT", [D, T])
    memT_in = din("memT", [4, D, M])
    rope_c = din("rope_c", [128, T])
    rope_s = din("rope_s", [128, T])
    masks_in = din("masks", [128, 20 * 512])
    dbias_in = din("dbias", [128, NB * 20])
    dbias1_in = din("dbias1", [128, NB * NKC])
    segflag_in = din("segflag", [128, 1])
    gains_in = din("gains", [128, L * 7 * 8])
    convw_in = din("convw", [128, L * 3 * 44])
    convb_in = din("convb", [128, L * 44])
    subg_in = din("subg", [128, L])
    lam_in = din("lamv", [128, L * 4 * 64])
    pm_in = din("pmat", [128, 128])
    w_qk = din("w_qk", [L, D, NQK])
    w_v = din("w_v", [L, D, NV])
    w_out = din("w_out", [L, D, D])
    w_xq = din("w_xq", [L, D, D])
    w_xkv = din("w_xkv", [L, D, 2 * D])
    w_xo = din("w_xo", [L, D, D])
    w_up = din("w_up", [L, D, NUP])
    w_down = din("w_down", [L, DFF, D])
    yT = nc.dram_tensor("yT", [D, T], F32, kind="ExternalOutput").ap()

    qkT = dscr("qkT", [NQK, T], BF16)
    vtok = dscr("vtok", [T, NV], BF16)
    oT = dscr("oT", [D, T], BF16)
    xmid = dscr("xmid", [D, T], F32)
    x1 = dscr("x1", [D, T], F32)
    h3T = dscr("h3T", [D, HW], BF16)
    wupb = dscr("wupb", [D, NUP], BF16)

    import contextlib

    with contextlib.ExitStack() as top:
        sems = {k: top.enter_context(nc.semaphore("s_" + k)) for k in ["pe", "act", "dve", "pool", "sp"]}
        dsems = [top.enter_context(nc.semaphore("d%d" % i)) for i in range(20)]
        S = Sched(nc, sems, dsems)
        pe, act, dve, pool, sp = nc.tensor, nc.scalar, nc.vector, nc.gpsimd, nc.sync

        uid = [0]

        def sb(es, name, shape, dt):
            uid[0] += 1
            return es.enter_context(nc.sbuf_tensor("s%d_%s" % (uid[0], name), list(shape), dt))

        def ps(es, name, shape, dt=F32):
            uid[0] += 1
            return es.enter_context(nc.psum_tensor("p%d_%s" % (uid[0], name), list(shape), dt))

        def load(dst_ap, src_ap, dstbuf, agent, srcbufs=()):
            S.op("sp", lambda: sp.dma_start(out=dst_ap, in_=src_ap), reads=list(srcbufs), writes=[dstbuf], agent=agent)

        def store(dst_ap, src_ap, srcbuf, agent, dstbufs=()):
            S.op("sp", lambda: sp.dma_start(out=dst_ap, in_=src_ap), reads=[srcbuf], writes=list(dstbufs), agent=agent)

        ones = sb(top, "ones", [128, 128], BF16)
        pmat = sb(top, "pmat", [128, 128], BF16)
        pmf = sb(top, "pmf", [128, 128], F32)
        epst = sb(top, "epst", [128, 1], F32)
        gains = sb(top, "gains", [128, L * 7 * 8], F32)
        convw = sb(top, "convw", [128, L * 3 * 44], F32)
        convb = sb(top, "convb", [128, L * 44], F32)
        subg = sb(top, "subg", [128, L], F32)
        lamv = sb(top, "lamv", [128, L * 4 * 64], F32)
        neglam = sb(top, "neglam", [128, L], F32)
        lamtmp = sb(top, "lamtmp", [128, 64], F32)
        lame = sb(top, "lame", [128, 4], F32)
        segflag = sb(top, "segflag", [128, 1], F32)
        negflag = sb(top, "negflag", [128, 1], F32)
        dbias = sb(top, "dbias", [128, NB * 20], F32)
        B_const = Buf()
        a_const = S.dma_agent()
        for t_, src in [(gains, gains_in), (convw, convw_in), (convb, convb_in), (subg, subg_in), (lamv, lam_in),
                        (segflag, segflag_in), (dbias, dbias_in), (pmf, pm_in)]:
            load(t_[:], src, B_const, a_const)
        B_c2 = Buf()
        S.op("pool", lambda: pool.memset(ones[:], 1.0), writes=[B_c2])
        S.op("pool", lambda: pool.memset(epst[:], EPS), writes=[B_c2])
        S.op("pool", lambda: pool.tensor_copy(out=pmat[:], in_=pmf[:]), reads=[B_const], writes=[B_c2])
        B_lam = Buf()
        for l in range(L):
            lam_init = 0.8 - 0.6 * float(np.exp(-0.3 * l))
            for j in range(2):
                a_ = lamv[:, (l * 4 + 2 * j) * 64:(l * 4 + 2 * j + 1) * 64]
                b_ = lamv[:, (l * 4 + 2 * j + 1) * 64:(l * 4 + 2 * j + 2) * 64]
                S.op("dve", lambda: dve.tensor_tensor(out=lamtmp[:], in0=a_, in1=b_, op=ALU.mult), reads=[B_const, B_lam], writes=[B_lam])
                S.op("dve", lambda: dve.tensor_reduce(out=lame[:, 2 * l + j:2 * l + j + 1], in_=lamtmp[:], axis=AX.X, op=ALU.add), reads=[B_lam], writes=[B_lam])
            S.op("act", lambda: act.activation(out=lame[:, 2 * l:2 * l + 2], in_=lame[:, 2 * l:2 * l + 2], func=AF.Exp), reads=[B_lam], writes=[B_lam])
            S.op("dve", lambda: dve.scalar_tensor_tensor(out=neglam[:, l:l + 1], in0=lame[:, 2 * l + 1:2 * l + 2], scalar=-lam_init,
                                                          in1=lame[:, 2 * l:2 * l + 1], op0=ALU.add, op1=ALU.subtract), reads=[B_lam], writes=[B_lam])
            S.op("dve", lambda: dve.tensor_scalar(out=subg[:, l:l + 1], in0=subg[:, l:l + 1], scalar1=1.0 - lam_init, scalar2=None, op0=ALU.mult),
                 reads=[B_const, B_lam], writes=[B_lam])
        S.op("dve", lambda: dve.tensor_scalar(out=negflag[:], in0=segflag[:], scalar1=-1.0, scalar2=None, op0=ALU.mult), reads=[B_const, B_lam], writes=[B_lam])
        CONST = [B_const, B_c2, B_lam]

        def gcol(l, which, kc):
            i = (l * 7 + which) * 8 + kc
            return gains[:, i:i + 1]

        h3v = h3T.rearrange("(kc p) t -> p kc t", p=128)
        B_h3pad = Buf()
        with contextlib.ExitStack() as es:
            z = sb(es, "zt", [128, 8, 512], BF16)
            Bz = Buf()
            S.op("pool", lambda: pool.memset(z[:], 0.0), writes=[Bz])
            az = S.dma_agent()
            store(h3v[:, :, 0:PAD], z[:, :, 0:PAD], Bz, az, [B_h3pad])
            c0 = PAD + T
            while c0 < HW:
                cw = min(512, HW - c0)
                store(h3v[:, :, c0:c0 + cw], z[:, :, 0:cw], Bz, az, [B_h3pad])
                c0 += cw
            S.barrier("sp", [B_h3pad])

        cast_rr = [0]

        def load_weight(dst, src, rows, cols, Bdst, stg, stgB, stgA, SW):
            nr = rows // 128
            srcv = src.rearrange("(kc p) n -> p kc n", p=128)
            for kc in range(nr):
                for c0 in range(0, cols, SW):
                    cw = min(SW, cols - c0)
                    i = cast_rr[0] % 2
                    cast_rr[0] += 1
                    load(stg[i][:, 0:cw], srcv[:, kc, c0:c0 + cw], stgB[i], stgA[i])
                    eng = ["pool", "dve"][i]
                    E = S.E[eng]
                    S.op(eng, lambda: E.tensor_copy(out=dst[:, kc, c0:c0 + cw], in_=stg[i][:, 0:cw]), reads=[stgB[i]], writes=[Bdst[i]])

        def rmsnorm_T(src, w, gl, gwhich, hT, sq, ssp, rstd, Bsrc, Bh, Bsq, Bss, Brs, nfree):
            S.op("act", lambda: act.activation(out=sq[:, :, 0:w], in_=src[:, :, 0:w], func=AF.Square), reads=Bsrc, writes=[Bsq])

            def mm():
                for kc in range(KC):
                    ins = pe.matmul(ssp[:, 0:w], lhsT=ones[:], rhs=sq[:, kc, 0:w], start=(kc == 0), stop=(kc == KC - 1))
                return ins
            S.op("pe", mm, reads=[Bsq] + CONST, writes=[Bss])
            S.op("act", lambda: act.activation(out=rstd[:, 0:w], in_=ssp[:, 0:w], func=AF.Ln, bias=epst[:], scale=1.0 / nfree), reads=[Bss] + CONST, writes=[Brs])
            S.op("act", lambda: act.activation(out=rstd[:, 0:w], in_=rstd[:, 0:w], func=AF.Exp, scale=-0.5), reads=[Brs], writes=[Brs])
            if hT is not None:
                for kc in range(KC):
                    k2 = 0
                    eng = "dve"
                    E = S.E[eng]
                    S.op(eng, lambda: E.scalar_tensor_tensor(out=hT[:, kc, 0:w], in0=src[:, kc, 0:w], scalar=gcol(gl, gwhich, kc), in1=rstd[:, 0:w],
                                                             op0=ALU.mult, op1=ALU.mult), reads=Bsrc + [Brs] + CONST, writes=[Bh[k2]])

        B_xdst_prev = []
        try:
            for l in range(L):
                chk('start')
                x_src = xT_in if l == 0 else x1
                x_dst = x1 if l == 0 else yT
                xsv = x_src.rearrange("(kc p) t -> p kc t", p=128)
                xdv = x_dst.rearrange("(kc p) t -> p kc t", p=128)
                xmv = xmid.rearrange("(kc p) t -> p kc t", p=128)
                qkv_ = qkT.rearrange("(fc p) t -> p fc t", p=128)
                oTv = oT.rearrange("(kc p) t -> p kc t", p=128)
                vtv = vtok.rearrange("(s p) n -> p s n", p=128)
                B_qk = [Buf() for _ in range(NB)]
                B_v = [Buf() for _ in range(NB)]
                B_oT = [Buf() for _ in range(NB)]
                B_xmid = [Buf() for _ in range(NB)]
                B_h3 = [Buf() for _ in range(NB)]
                B_xdst = Buf()
                B_wup = Buf()

                S.full_barrier()
                S.next = 2
                with contextlib.ExitStack() as es:
                    Wqk = sb(es, "Wqk", [128, KC, NQK], BF16)
                    Wv = sb(es, "Wv", [128, KC, NV], BF16)
                    stg = [sb(es, "stg%d" % i, [128, 2048], F32) for i in range(2)]
                    stgB = [Buf(), Buf()]
                    stgA = [S.dma_agent(), S.dma_agent()]
                    BW = [Buf(), Buf()]
                    load_weight(Wqk, w_qk[l], D, NQK, BW, stg, stgB, stgA, 2048)
                    load_weight(Wv, w_v[l], D, NV, BW, stg, stgB, stgA, 2048)
                    chk('A0')
                    xb = [sb(es, "xb%d" % i, [128, KC, 512], F32) for i in range(2)]
                    cb = [sb(es, "cb%d" % i, [128, 512], F32) for i in range(2)]
                    sbt = [sb(es, "sbt%d" % i, [128, 512], F32) for i in range(2)]
                    Bxb = [Buf(), Buf()]
                    Bcb = [Buf(), Buf()]
                    Axb = [S.dma_agent(), S.dma_agent()]
                    Acb = [S.dma_agent(), S.dma_agent()]
                    sq = sb(es, "sq", [128, KC, 512], BF16)
                    rstd = sb(es, "rstd", [128, 512], F32)
                    hT = sb(es, "hT", [128, KC, 512], BF16)
                    t1 = [sb(es, "t1_%d" % i, [128, 512], F32) for i in range(2)]
                    t2 = [sb(es, "t2_%d" % i, [128, 512], F32) for i in range(2)]
                    qbf = [sb(es, "qbf%d" % i, [128, 512], BF16) for i in range(2)]
                    qko = sb(es, "qko", [128, 16, 512], BF16)
                    vo = sb(es, "vo", [128, 4, NV], BF16)
                    Bsq, Brs = Buf(), Buf()
                    Bh = [Buf(), Buf()]
                    Bt1 = [Buf(), Buf()]
                    Bt2 = [Buf(), Buf()]
                    Bqbf = [Buf(), Buf()]
                    Bqko, Bvo = Buf(), Buf()
                    Aqko, Avo = S.dma_agent(), S.dma_agent()
                    ssp = ps(es, "ssp", [128, 512])
                    pa = [ps(es, "pa%d" % i, [128, 512]) for i in range(2)]
                    pb = [ps(es, "pb%d" % i, [128, 512]) for i in range(2)]
                    pv = [ps(es, "pv%d" % i, [128, 512]) for i in range(2)]
                    Bss = PB()
                    Bpa = [PB(), PB()]
                    Bpb = [PB(), PB()]
                    Bpv = [PB(), PB()]

                    def issue_loads(b):
                        i = b % 2
                        load(xb[i][:], xsv[:, :, b * 512:(b + 1) * 512], Bxb[i], Axb[i], B_xdst_prev)
                        load(cb[i][:], rope_c[:, b * 512:(b + 1) * 512], Bcb[i], Acb[i])
                        load(sbt[i][:], rope_s[:, b * 512:(b + 1) * 512], Bcb[i], Acb[i])

                    issue_loads(0)
                    cnt = 0
                    for b in range(NB):
                        i = b % 2
                        if b + 1 < NB:
                            issue_loads(b + 1)
                        rmsnorm_T(xb[i], 512, l, 0, hT, sq, ssp, rstd, [Bxb[i]], Bh, Bsq, Bss, Brs, D)
                        chk('A1')
                        def emit_mma(fc_):
                            j_ = fc_ % 2

                            def mma():
                                for kc in range(KC):
                                    ins = pe.matmul(pa[j_][:], lhsT=Wqk[:, kc, fc_ * 128:(fc_ + 1) * 128], rhs=hT[:, kc, :], start=(kc == 0), stop=(kc == KC - 1))
                                return ins
                            S.op("pe", mma, reads=Bh + BW, writes=[Bpa[j_]])

                        emit_mma(0)
                        for fc in range(16):
                            j = fc % 2
                            if fc + 1 < 16:
                                emit_mma(fc + 1)
                            chk('A2a')
                            S.op("act", lambda: act.activation(out=qbf[j][:], in_=pa[j][:], func=AF.Copy), reads=[Bpa[j]], writes=[Bqbf[j]])
                            chk('A2b')
                            S.op("pe", lambda: pe.matmul(pb[j][:], lhsT=pmat[:], rhs=qbf[j][:], start=True, stop=True), reads=[Bqbf[j]] + CONST, writes=[Bpb[j]])
                            chk('A2c')
                            S.op("dve", lambda: dve.tensor_tensor(out=t1[j][:], in0=pa[j][:], in1=cb[i][:], op=ALU.mult), reads=[Bpa[j], Bcb[i]], writes=[Bt1[j]])
                            chk('A2t1')
                            S.op("dve", lambda: dve.tensor_tensor(out=t2[j][:], in0=pb[j][:], in1=sbt[i][:], op=ALU.mult), reads=[Bpb[j], Bcb[i]], writes=[Bt2[j]])
                            chk('A2d')
                            S.op("pool", lambda: pool.tensor_tensor(out=qko[:, fc, :], in0=t1[j][:], in1=t2[j][:], op=ALU.add), reads=[Bt1[j], Bt2[j]], writes=[Bqko])
                        chk('A2')
                        store(qkv_[:, :, b * 512:(b + 1) * 512], qko[:], Bqko, Aqko, [B_qk[b]])
                        chk('A3')
                        for ts in range(4):
                            for hf in range(2):
                                j = cnt % 2
                                cnt += 1

                                def mmv():
                                    for kc in range(KC):
                                        ins = pe.matmul(pv[j][:], lhsT=hT[:, kc, ts * 128:(ts + 1) * 128], rhs=Wv[:, kc, hf * 512:(hf + 1) * 512], start=(kc == 0), stop=(kc == KC - 1))
                                    return ins
                                S.op("pe", mmv, reads=Bh + BW, writes=[Bpv[j]])
                                S.op("act", lambda: act.activation(out=vo[:, ts, hf * 512:(hf + 1) * 512], in_=pv[j][:], func=AF.Copy), reads=[Bpv[j]], writes=[Bvo])
                        store(vtv[:, b * 4:(b + 1) * 4, :], vo[:], Bvo, Avo, [B_v[b]])

                chk('A')
                S.full_barrier()
                S.next = 2
                with contextlib.ExitStack() as es:
                    KT = sb(es, "KT", [128, T], BF16)
                    Vh = sb(es, "Vh", [128, NKC, 128], BF16)
                    db1 = sb(es, "db1", [128, NB * NKC], F32)
                    BK, BV, Bdb = Buf(), Buf(), Buf()
                    AK, AV, Adb = S.dma_agent(), S.dma_agent(), S.dma_agent()
                    load(db1[:], dbias1_in, Bdb, Adb)
                    QT = [sb(es, "QT%d" % i, [128, 512], BF16) for i in range(2)]
                    BQ = [Buf(), Buf()]
                    AQ = [S.dma_agent(), S.dma_agent()]
                    NPT = 8
                    pT = [sb(es, "pT%d" % i, [128, 2, 512], BF16) for i in range(NPT)]
                    BpT = [Buf() for _ in range(NPT)]
                    sT = [ps(es, "sT%d" % i, [128, 2, 512]) for i in range(2)]
                    BsT = [PB(), PB()]
                    accO = ps(es, "accO", [128, 2, 512])
                    accS = ps(es, "accS", [128, 2, 512])
                    Bacc = PB()
                    rr = sb(es, "rr", [128, 2, 512], F32)
                    sacc = sb(es, "sacc", [128, 512], F32)
                    aS = sb(es, "aS", [128, 2, 512], F32)
                    aO = sb(es, "aO", [128, 2, 512], F32)
                    Bs0 = PB()
                    shi = sb(es, "shi", [128, 512], BF16)
                    slo = sb(es, "slo", [128, 512], BF16)
                    sdf = sb(es, "sdf", [128, 512], F32)
                    Bsacc, Bshl = Buf(), Buf()
                    o1 = sb(es, "o1", [128, 512], F32)
                    o2 = sb(es, "o2", [128, 512], F32)
                    osq = sb(es, "osq", [128, 512], BF16)
                    rs = sb(es, "rs", [128, 512], F32)
                    ob = [sb(es, "ob%d" % i, [128, 512], BF16) for i in range(2)]
                    Bfin = Buf()
                    Bob = [Buf(), Buf()]
                    Aob = [S.dma_agent(), S.dma_agent()]
                    nstep = [0]
                    for h in range(4):
                        load(KT[:], qkT[512 + h * 128:512 + (h + 1) * 128, :], BK, AK, B_qk)
                        for vp in range(0, NKC, 16):
                            load(Vh[:, vp:vp + 16, :], vtv[:, vp:vp + 16, h * 128:(h + 1) * 128], BV, AV, B_v)
                        steps = [(qb, kc) for qb in range(NB) for kc in range(NKC)]
                        base = nstep[0]

                        def load_q(qb):
                            qi = qb % 2
                            load(QT[qi][:], qkT[h * 128:(h + 1) * 128, qb * 512:(qb + 1) * 512], BQ[qi], AQ[qi], [B_qk[qb]])

                        def emit_qk(n):
                            qb, kc = steps[n]
                            si = (base + n) % 2
                            qi = qb % 2
                            if kc == min(1, NKC - 1) and qb + 1 < NB:
                                load_q(qb + 1)

                            def qk():
                                pe.matmul(sT[si][:, 0, :], lhsT=KT[0:64, kc * 128:(kc + 1) * 128], rhs=QT[qi][0:64, :], start=True, stop=True, tile_position=(0, 0))
                                return pe.matmul(sT[si][:, 1, :], lhsT=KT[64:128, kc * 128:(kc + 1) * 128], rhs=QT[qi][64:128, :], start=True, stop=True, tile_position=(64, 0))
                            S.op("pe", qk, reads=[BK, BQ[qi]], writes=[BsT[si]])

                        load_q(0)
                        emit_qk(0)
                        if len(steps) > 1:
                            emit_qk(1)
                        for n, (qb, kc) in enumerate(steps):
                            si = (base + n) % 2
                            bi = qb * NKC + kc
                            pi = (base + n) % NPT
                            S.op("act", lambda: act.activation(out=pT[pi][:], in_=sT[si][:], func=AF.Exp, bias=db1[:, bi:bi + 1], scale=0.125), reads=[BsT[si], Bdb], writes=[BpT[pi]])
                            if n + 2 < len(steps):
                                emit_qk(n + 2)
                            if kc == 0:
                                S.op("dve", lambda: dve.tensor_copy(out=sacc[:], in_=pT[pi][:, 0, :]), reads=[BpT[pi]], writes=[Bsacc])
                            else:
                                S.op("dve", lambda: dve.tensor_tensor(out=sacc[:], in0=sacc[:], in1=pT[pi][:, 0, :], op=ALU.add), reads=[BpT[pi], Bsacc], writes=[Bsacc])

                            def pvm():
                                pe.matmul(accO[:, 0, :], lhsT=Vh[:, kc, :], rhs=pT[pi][:, 0, :], start=(kc == 0), stop=(kc == NKC - 1))
                                pe.matmul(accO[:, 1, :], lhsT=Vh[:, kc, :], rhs=pT[pi][:, 1, :], start=(kc == 0), stop=(kc == NKC - 1))
                                return pe.matmul(accS[:, 1, :], lhsT=ones[:], rhs=pT[pi][:, 1, :], start=(kc == 0), stop=(kc == NKC - 1))
                            S.op("pe", pvm, reads=[BV, BpT[pi]] + CONST, writes=[Bacc])
                            if kc != NKC - 1:
                                continue
                            oi = qb % 2
                            S.op("dve", lambda: dve.tensor_copy(out=shi[:], in_=sacc[:]), reads=[Bsacc, Bshl], writes=[Bshl])
                            S.op("dve", lambda: dve.tensor_tensor(out=sdf[:], in0=sacc[:], in1=shi[:], op=ALU.subtract), reads=[Bsacc, Bshl], writes=[Bshl])
                            S.op("dve", lambda: dve.tensor_copy(out=slo[:], in_=sdf[:]), reads=[Bshl], writes=[Bshl])

                            def sfin():
                                pe.matmul(accS[:, 0, :], lhsT=ones[:], rhs=shi[:], start=True, stop=False)
                                return pe.matmul(accS[:, 0, :], lhsT=ones[:], rhs=slo[:], start=False, stop=True)
                            S.op("pe", sfin, reads=[Bshl] + CONST, writes=[Bs0])
                            S.op("dve", lambda: dve.tensor_copy(out=aS[:], in_=accS[:]), reads=[Bacc, Bs0, Bfin], writes=[Bfin])
                            S.op("dve", lambda: dve.tensor_copy(out=aO[:], in_=accO[:]), reads=[Bacc, Bfin], writes=[Bfin])
                            S.op("dve", lambda: dve.reciprocal(out=rr[:], in_=aS[:]), reads=[Bfin], writes=[Bfin])
                            S.op("dve", lambda: dve.tensor_tensor(out=o1[:], in0=aO[:, 0, :], in1=rr[:, 0, :], op=ALU.mult), reads=[Bfin], writes=[Bfin])
                            S.op("dve", lambda: dve.tensor_tensor(out=o2[:], in0=aO[:, 1, :], in1=rr[:, 1, :], op=ALU.mult), reads=[Bfin], writes=[Bfin])
                            S.op("dve", lambda: dve.scalar_tensor_tensor(out=o1[:], in0=o2[:], scalar=neglam[:, l:l + 1], in1=o1[:], op0=ALU.mult, op1=ALU.add),
                                 reads=[Bfin] + CONST, writes=[Bfin])
                            S.op("act", lambda: act.activation(out=osq[:], in_=o1[:], func=AF.Square), reads=[Bfin], writes=[Bfin])
                            S.op("pe", lambda: pe.matmul(accS[:, 0, :], lhsT=ones[:], rhs=osq[:], start=True, stop=True), reads=[Bfin] + CONST, writes=[Bs0])
                            S.op("act", lambda: act.activation(out=rs[:], in_=accS[:, 0, :], func=AF.Ln, bias=epst[:], scale=1.0 / 128), reads=[Bs0] + CONST, writes=[Bfin])
                            S.op("act", lambda: act.activation(out=rs[:], in_=rs[:], func=AF.Exp, scale=-0.5), reads=[Bfin], writes=[Bfin])
                            S.op("dve", lambda: dve.scalar_tensor_tensor(out=ob[oi][:], in0=o1[:], scalar=subg[:, l:l + 1], in1=rs[:], op0=ALU.mult, op1=ALU.mult),
                                 reads=[Bfin] + CONST, writes=[Bob[oi]])
                            store(oT[h * 128:(h + 1) * 128, qb * 512:(qb + 1) * 512], ob[oi][:], Bob[oi], Aob[oi], [B_oT[qb]])
                        nstep[0] += len(steps)

                chk('C1')
                S.full_barrier()
                S.next = 2
                with contextlib.ExitStack() as es:
                    KT = sb(es, "KTd", [128, T], BF16)
                    Vh = sb(es, "Vhd", [128, NKC, 128], BF16)
                    masks = sb(es, "masks", [128, 20, 512], BF16)
                    BK, BV, BM = Buf(), Buf(), Buf()
                    AK, AV = S.dma_agent(), S.dma_agent()
                    with contextlib.ExitStack() as es2:
                        stg = [sb(es2, "mstg%d" % i, [128, 2048], F32) for i in range(2)]
                        stgB = [Buf(), Buf()]
                        stgA = [S.dma_agent(), S.dma_agent()]
                        for g in range(5):
                            i = g % 2
                            load(stg[i][:], masks_in[:, g * 2048:(g + 1) * 2048], stgB[i], stgA[i])
                            S.op("pool", lambda: pool.tensor_copy(out=masks[:, g * 4:(g + 1) * 4, :], in_=stg[i][:].rearrange("p (a b) -> p a b", a=4)), reads=[stgB[i]], writes=[BM])
                    S.full_barrier()
                    QT = [sb(es, "QTd%d" % i, [128, 512], BF16) for i in range(2)]
                    BQ = [Buf(), Buf()]
                    AQ = [S.dma_agent(), S.dma_agent()]
                    NET = 4
                    eT = [sb(es, "eT%d" % i, [128, 2, 512], BF16) for i in range(NET)]
                    emT = [sb(es, "emT%d" % i, [128, 2, 512], BF16) for i in range(NET)]
                    BeT = [Buf() for _ in range(NET)]
                    BemT = [Buf() for _ in range(NET)]
                    sT = [ps(es, "sTd%d" % i, [128, 2, 512]) for i in range(2)]
                    BsT = [PB(), PB()]
                    accO = [ps(es, "accOd%d" % i, [128, 512]) for i in range(2)]
                    accS = [ps(es, "accSd%d" % i, [128, 512]) for i in range(2)]
                    Bacc = [PB(), PB()]
                    rr = [sb(es, "rrd%d" % i, [128, 512], F32) for i in range(2)]
                    Brr = [Buf(), Buf()]
                    ob = [sb(es, "obd%d" % i, [128, 512], BF16) for i in range(2)]
                    Bob = [Buf(), Buf()]
                    Aob = [S.dma_agent(), S.dma_agent()]
                    nstep = [0]
                    gcnt = [0]
                    for hp in range(4):
                        load(KT[:], qkT[1536 + hp * 128:1536 + (hp + 1) * 128, :], BK, AK, B_qk)
                        for vp in range(0, NKC, 16):
                            load(Vh[:, vp:vp + 16, :], vtv[:, vp:vp + 16, 512 + hp * 128:512 + (hp + 1) * 128], BV, AV, B_v)
                        steps = []
                        for qb in range(NB):
                            dl = [d_ for d_ in range(-8, 12) if 0 <= 4 * qb + d_ < NKC]
                            g_ = gcnt[0]
                            gcnt[0] += 1
                            for n_, d_ in enumerate(dl):
                                steps.append((qb, d_, n_ == 0, n_ == len(dl) - 1, g_ % 2))
                        base = nstep[0]

                        def load_q(qb):
                            qi = qb % 2
                            load(QT[qi][:], qkT[1024 + hp * 128:1024 + (hp + 1) * 128, qb * 512:(qb + 1) * 512], BQ[qi], AQ[qi], [B_qk[qb]])

                        def emit_qk(n):
                            qb, d_, first, last, ai = steps[n]
                            kc = 4 * qb + d_
                            si = (base + n) % 2
                            qi = qb % 2
                            if first and qb + 1 < NB:
                                load_q(qb + 1)

                            def qk():
                                pe.matmul(sT[si][:, 0, :], lhsT=KT[0:64, kc * 128:(kc + 1) * 128], rhs=QT[qi][0:64, :], start=True, stop=True, tile_position=(0, 0))
                                return pe.matmul(sT[si][:, 1, :], lhsT=KT[64:128, kc * 128:(kc + 1) * 128], rhs=QT[qi][64:128, :], start=True, stop=True, tile_position=(64, 0))
                            S.op("pe", qk, reads=[BK, BQ[qi]], writes=[BsT[si]])

                        load_q(0)
                        for n in range(min(2, len(steps))):
                            emit_qk(n)
                        for n, (qb, d_, first, last, ai) in enumerate(steps):
                            kc = 4 * qb + d_
                            si = (base + n) % 2
                            bi = qb * 20 + d_ + 8
                            ei = (base + n) % NET
                            S.op("act", lambda: act.activation(out=eT[ei][:], in_=sT[si][:], func=AF.Exp, bias=dbias[:, bi:bi + 1], scale=0.125),
                                 reads=[BsT[si]] + CONST, writes=[BeT[ei]])
                            if n + 2 < len(steps):
                                emit_qk(n + 2)

                            def mk():
                                dve.tensor_tensor(out=emT[ei][:, 0, :], in0=eT[ei][:, 0, :], in1=masks[:, d_ + 8, :], op=ALU.mult)
                                return dve.tensor_tensor(out=emT[ei][:, 1, :], in0=eT[ei][:, 1, :], in1=masks[:, d_ + 8, :], op=ALU.mult)
                            S.op("dve", mk, reads=[BeT[ei], BM], writes=[BemT[ei]])

                            def pvm():
                                for hh in range(2):
                                    pe.matmul(accO[ai][hh * 64:(hh + 1) * 64, :], lhsT=Vh[:, kc, hh * 64:(hh + 1) * 64], rhs=emT[ei][:, hh, :], start=first, stop=last)
                                for hh in range(2):
                                    ins = pe.matmul(accS[ai][hh * 64:(hh + 1) * 64, :], lhsT=ones[:, hh * 64:(hh + 1) * 64], rhs=emT[ei][:, hh, :], start=first, stop=last)
                                return ins
                            S.op("pe", pvm, reads=[BV, BemT[ei]] + CONST, writes=[Bacc[ai]])
                            if not last:
                                continue
                            S.op("dve", lambda: dve.reciprocal(out=rr[ai][:], in_=accS[ai][:]), reads=[Bacc[ai]], writes=[Brr[ai]])
                            S.op("dve", lambda: dve.tensor_tensor(out=ob[ai][:], in0=accO[ai][:], in1=rr[ai][:], op=ALU.mult), reads=[Bacc[ai], Brr[ai]], writes=[Bob[ai]])
                            r0 = 512 + hp * 128
                            store(oT[r0:r0 + 128, qb * 512:(qb + 1) * 512], ob[ai][:], Bob[ai], Aob[ai], [B_oT[qb]])
                        nstep[0] += len(steps)

                chk('C2')
                S.full_barrier()
                S.next = 2
                with contextlib.ExitStack() as es:
                    Kmem = sb(es, "Kmem", [128, 4, 8, M], BF16)
                    Vmem = sb(es, "Vmem", [128, 4, 2, D], BF16)
                    BKV = Buf()
                    with contextlib.ExitStack() as es2:
                        Wkv = sb(es2, "Wkv", [128, KC, 2 * D], BF16)
                        stg = [sb(es2, "kstg%d" % i, [128, 2048], F32) for i in range(2)]
                        stgB = [Buf(), Buf()]
                        stgA = [S.dma_agent(), S.dma_agent()]
                        BWkv = [Buf(), Buf()]
                        load_weight(Wkv, w_xkv[l], D, 2 * D, BWkv, stg, stgB, stgA, 2048)
                        mb = sb(es2, "mb", [128, KC, M], F32)
                        mh = sb(es2, "mh", [128, KC, M], BF16)
                        sq = sb(es2, "ksq", [128, KC, M], BF16)
                        rstd = sb(es2, "krstd", [128, M], F32)
                        ssp = ps(es2, "kssp", [128, 512])
                        pm = [ps(es2, "kpm%d" % i, [128, 512]) for i in range(2)]
                        Bpm = [PB(), PB()]
                        Bmb, Bsq, Bss, Brs = Buf(), Buf(), PB(), Buf()
                        Bmh = [Buf(), Buf()]
                        Amb = S.dma_agent()
                        cnt = 0
                        for s_ in range(4):
                            load(mb[:], memT_in[s_].rearrange("(kc p) m -> p kc m", p=128), Bmb, Amb)
                            rmsnorm_T(mb, M, l, 2, mh, sq, ssp, rstd, [Bmb], Bmh, Bsq, Bss, Brs, D)
                            for fc in range(8):
                                j = cnt % 2
                                cnt += 1

                                def mmk():
                                    for kc in range(KC):
                                        ins = pe.matmul(pm[j][:, 0:M], lhsT=Wkv[:, kc, fc * 128:(fc + 1) * 128], rhs=mh[:, kc, :], start=(kc == 0), stop=(kc == KC - 1))
                                    return ins
                                S.op("pe", mmk, reads=Bmh + BWkv, writes=[Bpm[j]])
                                S.op("act", lambda: act.activation(out=Kmem[:, s_, fc, :], in_=pm[j][:, 0:M], func=AF.Copy), reads=[Bpm[j]], writes=[BKV])
                            for mc in range(2):
                                for hf in range(2):
                                    j = cnt % 2
                                    cnt += 1

                                    def mmvv():
                                        for kc in range(KC):
                                            ins = pe.matmul(pm[j][:], lhsT=mh[:, kc, mc * 128:(mc + 1) * 128], rhs=Wkv[:, kc, D + hf * 512:D + (hf + 1) * 512], start=(kc == 0), stop=(kc == KC - 1))
                                        return ins
                                    S.op("pe", mmvv, reads=Bmh + BWkv, writes=[Bpm[j]])
                                    S.op("act", lambda: act.activation(out=Vmem[:, s_, mc, hf * 512:(hf + 1) * 512], in_=pm[j][:], func=AF.Copy), reads=[Bpm[j]], writes=[BKV])

                    S.full_barrier()
                    Wo = sb(es, "Wo", [128, KC, D], BF16)
                    Wq = sb(es, "Wq", [128, KC, D], BF16)
                    Wx = sb(es, "Wx", [128, KC, D], BF16)
                    BW = [Buf(), Buf()]
                    stg = [sb(es, "dstg%d" % i, [128, 1024], F32) for i in range(2)]
                    stgB = [Buf(), Buf()]
                    stgA = [S.dma_agent(), S.dma_agent()]
                    load_weight(Wo, w_out[l], D, D, BW, stg, stgB, stgA, 1024)
                    load_weight(Wq, w_xq[l], D, D, BW, stg, stgB, stgA, 1024)
                    load_weight(Wx, w_xo[l], D, D, BW, stg, stgB, stgA, 1024)
                    xb = sb(es, "dxb", [128, KC, 512], F32)
                    ob_ = sb(es, "dob", [128, KC, 512], BF16)
                    Bxb, Bob_ = Buf(), Buf()
                    Axb, Aob_, Axs = S.dma_agent(), S.dma_agent(), S.dma_agent()
                    sqox = sb(es, "dsqox", [128, KC, 512], BF16)
                    rstd = sb(es, "drstd", [128, 512], F32)
                    hT = sb(es, "dhT", [128, KC, 512], BF16)
                    mf = sb(es, "mf", [128, KC, 512], F32)
                    tmp = [sb(es, "dtmp%d" % i, [128, 512], F32) for i in range(2)]
                    qx = sb(es, "qx", [128, KC, 512], BF16)
                    pxx = sb(es, "pxx", [128, 2, 512], BF16)
                    rx = sb(es, "rx", [128, 512], F32)
                    Ah3 = S.dma_agent()
                    Bsqox, Brs, Bmf, Bqx, Bpx, Brx = Buf(), Buf(), Buf(), Buf(), Buf(), Buf()
                    Bh = [Buf(), Buf()]
                    Btmp = [Buf(), Buf()]
                    ssp = ps(es, "dssp", [128, 512])
                    pm = [ps(es, "pm%d" % i, [128, 512]) for i in range(2)]
                    psx = [ps(es, "psx%d" % i, [128, 512]) for i in range(2)]
                    pox = [ps(es, "pox%d" % i, [128, 512]) for i in range(2)]
                    pls = ps(es, "pls", [128, 512])
                    Bss, Bls = PB(), PB()
                    Bpm = [PB(), PB()]
                    Bpsx = [PB(), PB()]
                    Bpox = [PB(), PB()]

                    def proj_post_add(Wt, src, Bsrc, gwhich):
                        for n_ in range(KC):
                            j = n_ % 2

                            def mm():
                                for kc in range(KC):
                                    ins = pe.matmul(pm[j][:], lhsT=Wt[:, kc, n_ * 128:(n_ + 1) * 128], rhs=src[:, kc, :], start=(kc == 0), stop=(kc == KC - 1))
                                return ins
                            S.op("pe", mm, reads=Bsrc + BW, writes=[Bpm[j]])
                            S.op("act", lambda: act.activation(out=mf[:, n_, :], in_=pm[j][:], func=AF.Copy), reads=[Bpm[j]], writes=[Bmf])
                        rmsnorm_T(mf, 512, l, 0, None, sqox, ssp, rstd, [Bmf], None, Bsqox, Bss, Brs, D)
                        for n_ in range(KC):
                            j = n_ % 2
                            S.op("dve", lambda: dve.scalar_tensor_tensor(out=tmp[j][:], in0=mf[:, n_, :], scalar=gcol(l, gwhich, n_), in1=rstd[:], op0=ALU.mult, op1=ALU.mult),
                                 reads=[Bmf, Brs] + CONST, writes=[Btmp[j]])
                            S.op("pool", lambda: pool.tensor_tensor(out=xb[:, n_, :], in0=xb[:, n_, :], in1=tmp[j][:], op=ALU.add), reads=[Btmp[j], Bxb], writes=[Bxb])

                    load(ob_[:], oTv[:, :, 0:512], Bob_, Aob_, [B_oT[0]])
                    for b in range(NB):
                        s_ = (b * 512) // SEG
                        load(xb[:], xsv[:, :, b * 512:(b + 1) * 512], Bxb, Axb, B_xdst_prev)
                        proj_post_add(Wo, ob_, [Bob_], 1)
                        if b + 1 < NB:
                            load(ob_[:], oTv[:, :, (b + 1) * 512:(b + 2) * 512], Bob_, Aob_, [B_oT[b + 1]])
                        rmsnorm_T(xb, 512, l, 3, hT, sqox, ssp, rstd, [Bxb], Bh, Bsqox, Bss, Brs, D)
                        for fc in range(KC):
                            j = fc % 2

                            def mmq():
                                for kc in range(KC):
                                    ins = pe.matmul(pm[j][:], lhsT=Wq[:, kc, fc * 128:(fc + 1) * 128], rhs=hT[:, kc, :], start=(kc == 0), stop=(kc == KC - 1))
                                return ins
                            S.op("pe", mmq, reads=Bh + BW, writes=[Bpm[j]])
                            S.op("act", lambda: act.activation(out=qx[:, fc, :], in_=pm[j][:], func=AF.Copy), reads=[Bpm[j]], writes=[Bqx])
                        for hx in range(4):
                            for mc in range(2):
                                def mms():
                                    for dc in range(2):
                                        ins = pe.matmul(psx[mc][:], lhsT=Kmem[:, s_, hx * 2 + dc, mc * 128:(mc + 1) * 128], rhs=qx[:, hx * 2 + dc, :], start=(dc == 0), stop=(dc == 1))
                                    return ins
                                S.op("pe", mms, reads=[Bqx, BKV], writes=[Bpsx[mc]])
                                S.op("act", lambda: act.activation(out=pxx[:, mc, :], in_=psx[mc][:], func=AF.Exp, scale=1.0 / 16.0), reads=[Bpsx[mc]], writes=[Bpx])

                            def mml():
                                for mc in range(2):
                                    ins = pe.matmul(pls[:], lhsT=ones[:], rhs=pxx[:, mc, :], start=(mc == 0), stop=(mc == 1))
                                return ins
                            S.op("pe", mml, reads=[Bpx] + CONST, writes=[Bls])
                            S.op("act", lambda: act.activation(out=rx[:], in_=pls[:], func=AF.Ln), reads=[Bls], writes=[Brx])
                            S.op("act", lambda: act.activation(out=rx[:], in_=rx[:], func=AF.Exp, scale=-1.0), reads=[Brx], writes=[Brx])
                            for dc in range(2):
                                def mmo():
                                    for mc in range(2):
                                        ins = pe.matmul(pox[dc][:], lhsT=Vmem[:, s_, mc, hx * 256 + dc * 128:hx * 256 + (dc + 1) * 128], rhs=pxx[:, mc, :], start=(mc == 0), stop=(mc == 1))
                                    return ins
                                S.op("pe", mmo, reads=[Bpx, BKV], writes=[Bpox[dc]])
                                S.op("dve", lambda: dve.tensor_tensor(out=sqox[:, hx * 2 + dc, :], in0=pox[dc][:], in1=rx[:], op=ALU.mult), reads=[Bpox[dc], Brx], writes=[Bsqox])
                        proj_post_add(Wx, sqox, [Bsqox], 4)
                        store(xmv[:, :, b * 512:(b + 1) * 512], xb[:], Bxb, Axs, [B_xmid[b]])
                        rmsnorm_T(xb, 512, l, 5, hT, sqox, ssp, rstd, [Bxb], Bh, Bsqox, Bss, Brs, D)
                        store(h3v[:, :, PAD + b * 512:PAD + (b + 1) * 512], hT[:], Bh[0], Ah3, [B_h3[b]])
                        S.barrier("sp", [Bh[1]])

                chk('D12')
                S.full_barrier()
                S.next = 2
                with contextlib.ExitStack() as es:
                    Wd = sb(es, "Wd", [128, NPAIR, D], BF16)
                    BWd = [Buf(), Buf()]
                    stg = [sb(es, "fstg%d" % i, [128, 1024], F32) for i in range(2)]
                    stgB = [Buf(), Buf()]
                    stgA = [S.dma_agent(), S.dma_agent()]
                    load_weight(Wd, w_down[l], DFF, D, BWd, stg, stgB, stgA, 1024)
                    cst = [sb(es, "cst%d" % i, [128, 1024], BF16) for i in range(2)]
                    cstB = [Buf(), Buf()]
                    cstA = [S.dma_agent(), S.dma_agent()]
                    wuv = w_up[l].rearrange("(kc p) n -> p kc n", p=128)
                    wubv = wupb.rearrange("(kc p) n -> p kc n", p=128)
                    cc = 0
                    for kc in range(KC):
                        for c0 in range(0, NUP, 1024):
                            cw = min(1024, NUP - c0)
                            i = cc % 2
                            cc += 1
                            load(stg[i][:, 0:cw], wuv[:, kc, c0:c0 + cw], stgB[i], stgA[i])
                            eng = ["pool", "dve"][i]
                            E = S.E[eng]
                            S.op(eng, lambda: E.tensor_copy(out=cst[i][:, 0:cw], in_=stg[i][:, 0:cw]), reads=[stgB[i]], writes=[cstB[i]])
                            store(wubv[:, kc, c0:c0 + cw], cst[i][:, 0:cw], cstB[i], cstA[i], [B_wup])
                    wu = [sb(es, "wu%d" % i, [128, KC, 2, 256], BF16) for i in range(2)]
                    Bwu = [Buf(), Buf()]
                    Awu = [S.dma_agent(), S.dma_agent()]
                    hb = sb(es, "hb", [128, KC, 512], BF16)
                    Bhb = Buf()
                    Ahb = S.dma_agent()
                    xb = sb(es, "fxb", [128, KC, 512], F32)
                    Bxb = Buf()
                    Axb, Axs = S.dma_agent(), S.dma_agent()
                    actT = sb(es, "actT", [128, NPAIR, 512], BF16)
                    Bact = Buf()
                    cg = [sb(es, "cg%d" % i, [128, 512], F32) for i in range(2)]
                    cv = [sb(es, "cv%d" % i, [128, 512], F32) for i in range(2)]
                    gg = [sb(es, "gg%d" % i, [128, 512], F32) for i in range(2)]
                    Bcg = [Buf(), Buf()]
                    Bcv = [Buf(), Buf()]
                    Bgg = [Buf(), Buf()]
                    fx = sb(es, "fx", [128, 1], F32)
                    Bfx = Buf()
                    mf = sb(es, "fmf", [128, KC, 512], F32)
                    rstd = sb(es, "frstd", [128, 512], F32)
                    tmp = [sb(es, "ftmp%d" % i, [128, 512], F32) for i in range(2)]
                    Bmf, Brs = Buf(), Buf()
                    Btmp = [Buf(), Buf()]
                    pug = [ps(es, "pug%d" % i, [128, 512]) for i in range(3)]
                    puv = [ps(es, "puv%d" % i, [128, 512]) for i in range(3)]
                    pf = [ps(es, "pf%d" % i, [128, 512]) for i in range(2)]
                    Bpug = [PB(), PB(), PB()]
                    Bpuv = [PB(), PB(), PB()]
                    Bpf = [PB(), PB()]
                    ssp = pug[0]
                    Bss = Bpug[0]
                    wcnt = 0

                    def cw_(t, f):
                        i = (l * 3 + t) * 44 + f
                        return convw[:, i:i + 1]

                    def cb_(f):
                        i = l * 44 + f
                        return convb[:, i:i + 1]

                    def wu_load(gi, slot):
                        load(wu[slot][:, :, 0, :], wubv[:, :, gi * 256:(gi + 1) * 256], Bwu[slot], Awu[slot], [B_wup])
                        load(wu[slot][:, :, 1, :], wubv[:, :, DFF + gi * 256:DFF + (gi + 1) * 256], Bwu[slot], Awu[slot], [B_wup])

                    wu_load(0, 0)
                    for b in range(NFB):
                        nout = min(FW, T - FW * b)
                        c_lo = PAD - 1 + FW * b
                        if b == 0:
                            load(hb[:], h3v[:, :, c_lo:c_lo + 512], Bhb, Ahb, B_h3 + [B_h3pad])
                        load(xb[:, :, 0:nout], xmv[:, :, FW * b:FW * b + nout], Bxb, Axb, B_xmid)
                        bounds = [SEG * k - FW * b for k in range(1, 4) if 0 <= SEG * k - FW * b <= FW]
                        for gi in range(11):
                            slot = wcnt % 2
                            wcnt += 1
                            if gi + 1 < 11:
                                wu_load(gi + 1, (slot + 1) % 2)
                            elif b + 1 < NFB:
                                wu_load(0, (slot + 1) % 2)
                            for jj in range(2):
                                pj = gi * 2 + jj
                                k_ = pj % 2

                                def mmu(dst, half):
                                    for kc in range(KC):
                                        ins = pe.matmul(dst[:], lhsT=wu[slot][:, kc, half, jj * 128:(jj + 1) * 128], rhs=hb[:, kc, :], start=(kc == 0), stop=(kc == KC - 1))
                                    return ins
                                p3 = pj % 3
                                S.op("pe", lambda: mmu(pug[p3], 0), reads=[Bwu[slot], Bhb], writes=[Bpug[p3]])
                                S.op("pe", lambda: mmu(puv[p3], 1), reads=[Bwu[slot], Bhb], writes=[Bpuv[p3]])
                                for (pu, Bpu, cdst, Bc, fo) in ((pug[p3], Bpug[p3], cg[k_], Bcg[k_], pj), (puv[p3], Bpuv[p3], cv[k_], Bcv[k_], NPAIR + pj)):
                                    S.op("act", lambda: act.activation(out=cdst[:, 0:FW], in_=pu[:, 1:1 + FW], func=AF.Identity, bias=cb_(fo), scale=cw_(1, fo)),
                                         reads=[Bpu] + CONST, writes=[Bc])
                                    S.op("dve", lambda: dve.scalar_tensor_tensor(out=cdst[:, 0:FW], in0=pu[:, 0:FW], scalar=cw_(0, fo), in1=cdst[:, 0:FW], op0=ALU.mult, op1=ALU.add),
                                         reads=[Bpu, Bc] + CONST, writes=[Bc])
                                    S.op("dve", lambda: dve.scalar_tensor_tensor(out=cdst[:, 0:FW], in0=pu[:, 2:2 + FW], scalar=cw_(2, fo), in1=cdst[:, 0:FW], op0=ALU.mult, op1=ALU.add),
                                         reads=[Bpu, Bc] + CONST, writes=[Bc])
                                    for jb in bounds:
                                        if jb < FW:
                                            S.op("dve", lambda: dve.tensor_scalar(out=fx[:], in0=pu[:, jb:jb + 1], scalar1=cw_(0, fo), scalar2=negflag[:, 0:1], op0=ALU.mult, op1=ALU.mult),
                                                 reads=[Bpu, Bfx] + CONST, writes=[Bfx])
                                            S.op("dve", lambda: dve.tensor_tensor(out=cdst[:, jb:jb + 1], in0=cdst[:, jb:jb + 1], in1=fx[:], op=ALU.add), reads=[Bfx, Bc], writes=[Bc])
                                        if jb >= 1:
                                            S.op("dve", lambda: dve.tensor_scalar(out=fx[:], in0=pu[:, jb + 1:jb + 2], scalar1=cw_(2, fo), scalar2=negflag[:, 0:1], op0=ALU.mult, op1=ALU.mult),
                                                 reads=[Bpu, Bfx] + CONST, writes=[Bfx])
                                            S.op("dve", lambda: dve.tensor_tensor(out=cdst[:, jb - 1:jb], in0=cdst[:, jb - 1:jb], in1=fx[:], op=ALU.add), reads=[Bfx, Bc], writes=[Bc])
                                S.op("act", lambda: act.activation(out=gg[k_][:, 0:FW], in_=cg[k_][:, 0:FW], func=AF.Gelu_apprx_tanh), reads=[Bcg[k_]], writes=[Bgg[k_]])
                                S.op("pool", lambda: pool.tensor_tensor(out=actT[:, pj, 0:FW], in0=gg[k_][:, 0:FW], in1=cv[k_][:, 0:FW], op=ALU.mult), reads=[Bgg[k_], Bcv[k_]], writes=[Bact])
                        if b + 1 < NFB:
                            c_nx = PAD - 1 + FW * (b + 1)
                            load(hb[:], h3v[:, :, c_nx:c_nx + 512], Bhb, Ahb, B_h3 + [B_h3pad])
                        for n_ in range(KC):
                            j = n_ % 2

                            def mmd():
                                for pj in range(NPAIR):
                                    ins = pe.matmul(pf[j][:, 0:FW], lhsT=Wd[:, pj, n_ * 128:(n_ + 1) * 128], rhs=actT[:, pj, 0:FW], start=(pj == 0), stop=(pj == NPAIR - 1))
                                return ins
                            S.op("pe", mmd, reads=[Bact] + BWd, writes=[Bpf[j]])
                            S.op("act", lambda: act.activation(out=mf[:, n_, 0:FW], in_=pf[j][:, 0:FW], func=AF.Copy), reads=[Bpf[j]], writes=[Bmf])
                        rmsnorm_T(mf, FW, l, 0, None, actT[:, 0:KC, :], ssp, rstd, [Bmf], None, Bact, Bss, Brs, D)
                        for n_ in range(KC):
                            j = n_ % 2
                            S.op("dve", lambda: dve.scalar_tensor_tensor(out=tmp[j][:, 0:FW], in0=mf[:, n_, 0:FW], scalar=gcol(l, 6, n_), in1=rstd[:, 0:FW], op0=ALU.mult, op1=ALU.mult),
                                 reads=[Bmf, Brs] + CONST, writes=[Btmp[j]])
                            S.op("pool", lambda: pool.tensor_tensor(out=xb[:, n_, 0:nout], in0=xb[:, n_, 0:nout], in1=tmp[j][:, 0:nout], op=ALU.add), reads=[Btmp[j], Bxb], writes=[Bxb])
                        store(xdv[:, :, FW * b:FW * b + nout], xb[:, :, 0:nout], Bxb, Axs, [B_xdst])
                B_xdst_prev = [B_xdst]

        except StopBuild:
            pass
        S.barrier("sp", B_xdst_prev)
    return nc


def _pmat():
    m = np.arange(128)
    pm = np.zeros((128, 128), np.float32)
    pm[(m // 64) * 64 + (m % 64 + 32) % 64, m] = 1.0
    return pm


def _rope_tables(pos):
    inv = (1.0 / (10000.0 ** (np.arange(0, 64, 2, dtype=np.float32) / np.float32(64)))).astype(np.float32)
    ang = (pos.astype(np.float32)[:, None] * inv[None, :]).astype(np.float32)
    c = np.cos(ang).astype(np.float32)
    s = np.sin(ang).astype(np.float32)
    C = np.concatenate([c, c, c, c], axis=1).T
    Sg = np.concatenate([-s, s, -s, s], axis=1).T
    return np.ascontiguousarray(C), np.ascontiguousarray(Sg)


def _dil_masks():
    kk = np.arange(128)[:, None]
    qq = np.arange(512)[None, :]
    out = np.zeros((128, 20, 512), np.float32)
    for di, d_ in enumerate(range(-8, 12)):
        delta = d_ * 128 + kk - qq
        m = (np.abs(delta) <= 64).astype(np.float32)
        m += ((np.abs(delta) <= 256) & (delta % 4 == 0)).astype(np.float32)
        m += ((np.abs(delta) <= 1024) & (delta % 16 == 0)).astype(np.float32)
        out[:, di, :] = m
    return out.reshape(128, 20 * 512)


def _prep_shared(w_in, w_out, lambda_q1, lambda_k1, lambda_q2, lambda_k2, diff_subln_g, w_xq, w_xkv, w_xo, w_up,
                 conv_w, conv_b, w_down, mix_pre_g, mix_post_g, mem_norm_g, xattn_pre_g, xattn_post_g, ffn_pre_g, ffn_post_g):
    f = np.float32
    qk_cols = np.concatenate([np.arange(0, 1024), np.arange(1536, 2560)])
    v_cols = np.concatenate([np.arange(1024, 1536), np.arange(2560, 3072)])
    w_qk = np.ascontiguousarray(w_in[:, :, qk_cols], dtype=f)
    w_v = np.ascontiguousarray(w_in[:, :, v_cols], dtype=f)
    gl = [mix_pre_g, mix_post_g, mem_norm_g, xattn_pre_g, xattn_post_g, ffn_pre_g, ffn_post_g]
    gains = np.zeros((128, L * 7 * 8), f)
    for l in range(L):
        for w_, g in enumerate(gl):
            gains[:, (l * 7 + w_) * 8:(l * 7 + w_ + 1) * 8] = np.asarray(g[l], f).reshape(8, 128).T
    convw = np.zeros((128, L * 3 * 44), f)
    convb = np.zeros((128, L * 44), f)
    for l in range(L):
        for t in range(3):
            convw[:, (l * 3 + t) * 44:(l * 3 + t + 1) * 44] = np.asarray(conv_w[l, t], f).reshape(44, 128).T
        convb[:, l * 44:(l + 1) * 44] = np.asarray(conv_b[l], f).reshape(44, 128).T
    subg = np.ascontiguousarray(np.asarray(diff_subln_g, f).T)
    lamv = np.zeros((128, L * 4 * 64), f)
    for l in range(L):
        for j, v in enumerate([lambda_q1, lambda_k1, lambda_q2, lambda_k2]):
            lamv[:, (l * 4 + j) * 64:(l * 4 + j + 1) * 64] = np.broadcast_to(np.asarray(v[l], f)[None, :], (128, 64))
    return dict(w_qk=w_qk, w_v=w_v, pmat=_pmat(), w_out=np.ascontiguousarray(w_out, f), w_xq=np.ascontiguousarray(w_xq, f),
                w_xkv=np.ascontiguousarray(w_xkv, f), w_xo=np.ascontiguousarray(w_xo, f), w_up=np.ascontiguousarray(w_up, f),
                w_down=np.ascontiguousarray(w_down, f), gains=gains, convw=convw, convb=convb, subg=subg, lamv=lamv,
                masks=_dil_masks())


def _unit_inputs(T, xs, mems, packed):
    f = np.float32
    NB = T // 512
    SEG = T // 4
    x = np.concatenate(xs, axis=0)
    assert x.shape[0] == T
    xT = np.ascontiguousarray(x.T, dtype=f)
    memT = np.ascontiguousarray(np.stack([m.T for m in mems], axis=0), dtype=f)
    if packed:
        pos = np.arange(T) % SEG
    else:
        pos = np.arange(T)
    C, Sg = _rope_tables(pos)
    dbias = np.zeros((128, NB * 20), f)
    if packed:
        for qb in range(NB):
            for di, d_ in enumerate(range(-8, 12)):
                kc = 4 * qb + d_
                if 0 <= kc < T // 128 and (kc * 128) // SEG != (qb * 512) // SEG:
                    dbias[:, qb * 20 + di] = NEG
    dbias1 = np.zeros((128, NB * (T // 128)), f)
    if packed:
        for qb in range(NB):
            for kc in range(T // 128):
                if (kc * 128) // SEG != (qb * 512) // SEG:
                    dbias1[:, qb * (T // 128) + kc] = NEG
    segflag = np.full((128, 1), 1.0 if packed else 0.0, f)
    return dict(xT=xT, memT=memT, rope_c=C, rope_s=Sg, dbias=dbias, dbias1=dbias1, segflag=segflag)


_NC_CACHE = {}


def run_units(T, units, shared):
    if T not in _NC_CACHE:
        _NC_CACHE[T] = build(T)
    nc = _NC_CACHE[T]
    in_maps = []
    for u in units:
        m = dict(shared)
        m.update(u)
        in_maps.append(m)
    res = run_bass_kernel_spmd(nc, in_maps, core_ids=list(range(len(in_maps))))
    return [np.asarray(r["yT"]) for r in res.results]


def kernel(x_prompt, x_sample, mem_prompt, mem_sample, **params):
    T = 16384
    x_prompt = np.asarray(x_prompt, np.float32)
    x_sample = np.asarray(x_sample, np.float32)
    mem_prompt = np.asarray(mem_prompt, np.float32)
    mem_sample = np.asarray(mem_sample, np.float32)
    shared = _prep_shared(**{k: np.asarray(v, np.float32) for k, v in params.items()})
    u_s0 = _unit_inputs(T, [x_sample[0]], [mem_sample[0]] * 4, False)
    u_s1 = _unit_inputs(T, [x_sample[1]], [mem_sample[1]] * 4, False)
    u_p = _unit_inputs(T, [x_prompt[i] for i in range(4)], [mem_prompt[i] for i in range(4)], True)
    units = [u_s0, u_s1, u_p] + [u_p] * 5
    ys = run_units(T, units, shared)
    y_sample = np.stack([ys[0].T, ys[1].T], axis=0).astype(np.float32)
    yp = ys[2].T
    y_prompt = np.ascontiguousarray(yp.reshape(4, 4096, D)).astype(np.float32)
    return (y_prompt, np.ascontiguousarray(y_sample))
```
